# Optimizing a Trainium2 kernel written in Bass

```python
import numpy as np
import jax, jax.numpy as jnp
from jax import lax


D_MODEL = 1024
BATCH = 16
SEQ = 2048
DEPTH = 4

GRID_W = 64
CTX_LEN = 256
N_MIXERS = 3
EPS = 1e-6
NEG_INF = -1e30

NA_HEADS = 16
NA_HEAD_DIM = D_MODEL // NA_HEADS
NA_WIN_ROWS = 8
NA_WIN_COLS = 16
NA_QBLOCK_W = 16
NA_KBLOCK_W = 2 * NA_WIN_COLS

GMLP_CHUNK = 128
GMLP_WIDTH = D_MODEL
GMLP_GROUPS = 8

POOL_WINDOWS = (2, 4, 8, 16)
POOL_GROUP = D_MODEL // len(POOL_WINDOWS)

FFN_HIDDEN = -(-8 * D_MODEL // (3 * 256)) * 256

N_A = len(range(0, DEPTH, N_MIXERS))
N_B = len(range(1, DEPTH, N_MIXERS))
N_C = len(range(2, DEPTH, N_MIXERS))

kernel_name = 'hybrid_na_gmlp_pool_prefix_dit'


def rmsnorm(x, g):
    xf = x.astype(jnp.float32)
    y = xf * lax.rsqrt(jnp.mean(xf * xf, axis=-1, keepdims=True) + EPS)
    return (y * g.astype(jnp.float32)).astype(x.dtype)


def layernorm(x, g, b):
    xf = x.astype(jnp.float32)
    mu = jnp.mean(xf, axis=-1, keepdims=True)
    xc = xf - mu
    var = jnp.mean(xc * xc, axis=-1, keepdims=True)
    return (xc * lax.rsqrt(var + EPS) * g.astype(jnp.float32) + b.astype(jnp.float32)).astype(x.dtype)


def swiglu(h, w_gate, w_up, w_down):
    return (jax.nn.silu(h @ w_gate) * (h @ w_up)) @ w_down


def _heads(t):
    B, L, _ = t.shape
    return t.reshape(B, L, NA_HEADS, NA_HEAD_DIM).transpose(0, 2, 1, 3)


def _na_column_tables():
    n_cb = GRID_W // NA_QBLOCK_W
    qcol = np.arange(GRID_W).reshape(n_cb, NA_QBLOCK_W)
    kc0 = np.clip(np.arange(n_cb) * NA_QBLOCK_W - NA_WIN_COLS // 2, 0, GRID_W - NA_KBLOCK_W)
    kcol = kc0[:, None] + np.arange(NA_KBLOCK_W)
    cstart = np.clip(qcol - NA_WIN_COLS // 2, 0, GRID_W - NA_WIN_COLS)
    col_ok = (kcol[:, None, :] >= cstart[..., None]) & (kcol[:, None, :] < cstart[..., None] + NA_WIN_COLS)
    dc_idx = np.clip(kcol[:, None, :] - qcol[:, :, None] + NA_WIN_COLS - 1, 0, 2 * NA_WIN_COLS - 2)
    return kcol, col_ok, dc_idx


def na_mixer(h, hc, w_qkv, w_o, q_g, k_g, rpb, ctx_out):
    B, S, _ = h.shape
    rows = S // GRID_W
    rw = min(NA_WIN_ROWS, rows)
    n_cb = GRID_W // NA_QBLOCK_W
    H, hd, QB, KB = NA_HEADS, NA_HEAD_DIM, NA_QBLOCK_W, NA_KBLOCK_W
    scale = hd ** -0.5
    kcol, col_ok, dc_idx = _na_column_tables()

    q, k, v = jnp.split(h @ w_qkv, 3, axis=-1)
    q, k, v = rmsnorm(_heads(q), q_g), rmsnorm(_heads(k), k_g), _heads(v)
    qc, kc, vc = jnp.split(hc @ w_qkv, 3, axis=-1)
    kc, vc = rmsnorm(_heads(kc), k_g), _heads(vc)

    kgrid = k.reshape(B, H, rows, GRID_W, hd)
    vgrid = v.reshape(B, H, rows, GRID_W, hd)
    q_rows = jnp.moveaxis(q.reshape(B, H, rows, GRID_W, hd), 2, 0)
    rpb_c = rpb[:, :, dc_idx]
    nloc = rw * KB
    ok = np.broadcast_to(col_ok[:, :, None, :], (n_cb, QB, rw, KB)).reshape(n_cb, QB, nloc)

    def row_block(args):
        q_r, r = args
        rs = jnp.clip(r - rw // 2, 0, rows - rw)

        def gather(t):
            t = lax.dynamic_slice_in_dim(t, rs, rw, axis=2)
            t = jnp.take(t, kcol, axis=3)
            return t.transpose(0, 1, 3, 2, 4, 5).reshape(B, H, n_cb, nloc, hd)

        kb, vb = gather(kgrid), gather(vgrid)
        qb = q_r.reshape(B, H, n_cb, QB, hd)
        dr = rs + jnp.arange(rw) - r + NA_WIN_ROWS - 1
        bias = jnp.take(rpb_c, dr, axis=1).transpose(0, 2, 3, 1, 4).reshape(H, n_cb, QB, nloc)
        s_loc = jnp.einsum('bhjqd,bhjkd->bhjqk', qb, kb).astype(jnp.float32) * scale + bias.astype(jnp.float32)
        s_loc = jnp.where(ok, s_loc, NEG_INF)
        s_ctx = jnp.einsum('bhjqd,bhkd->bhjqk', qb, kc).astype(jnp.float32) * scale
        p = jax.nn.softmax(jnp.concatenate([s_loc, s_ctx], axis=-1), axis=-1).astype(v.dtype)
        o = (jnp.einsum('bhjqk,bhjkd->bhjqd', p[..., :nloc], vb)
             + jnp.einsum('bhjqk,bhkd->bhjqd', p[..., nloc:], vc))
        return o.reshape(B, H, GRID_W, hd)

    o = lax.map(row_block, (q_rows, jnp.arange(rows)))
    y = o.transpose(1, 0, 3, 2, 4).reshape(B, S, H * hd) @ w_o
    if not ctx_out:
        return y, None
    qc = rmsnorm(_heads(qc), q_g)
    sc = jnp.einsum('bhqd,bhkd->bhqk', qc, kc).astype(jnp.float32) * scale
    pc = jax.nn.softmax(sc, axis=-1).astype(vc.dtype)
    oc = jnp.einsum('bhqk,bhkd->bhqd', pc, vc)
    Bc, _, Lc, _ = oc.shape
    yc = oc.transpose(0, 2, 1, 3).reshape(Bc, Lc, H * hd) @ w_o
    return y, yc


def gmlp_mixer(h, w_in, b_in, ln_g, ln_b, w_s, b_s, w_out):
    B, L, _ = h.shape
    z = jax.nn.gelu(h @ w_in + b_in)
    u, v = jnp.split(z, 2, axis=-1)
    v = layernorm(v, ln_g, ln_b)
    vg = v.reshape(B, L // GMLP_CHUNK, GMLP_CHUNK, GMLP_GROUPS, GMLP_WIDTH // GMLP_GROUPS)
    mixed = jnp.einsum('gpq,bnqgd->bnpgd', w_s, vg) + b_s.T[None, None, :, :, None]
    return (u * mixed.reshape(B, L, GMLP_WIDTH)) @ w_out


def pool_mixer(h, w_pool, pool_scale):
    B, L, _ = h.shape
    t = np.arange(L)
    outs = []
    for g, w in enumerate(POOL_WINDOWS):
        hg = h[..., g * POOL_GROUP:(g + 1) * POOL_GROUP].astype(jnp.float32)
        cs = jnp.pad(jnp.cumsum(hg, axis=1), ((0, 0), (1, 0), (0, 0)))
        lo = np.clip(t - w // 2, 0, L)
        hi = np.clip(t - w // 2 + w, 0, L)
        cnt = (hi - lo).astype(np.float32)[:, None]
        mean = (jnp.take(cs, hi, axis=1) - jnp.take(cs, lo, axis=1)) / cnt
        outs.append(jnp.einsum('bld,de->ble', (mean - hg).astype(h.dtype), w_pool[g]))
    return jnp.concatenate(outs, axis=-1) * pool_scale


def setup_inputs(seed: int = 0) -> dict:
    key = jax.random.key(seed)
    ks = jax.random.split(key, 26)
    D, F, H, hd = D_MODEL, FFN_HIDDEN, NA_HEADS, NA_HEAD_DIM

    def nrm(k, shape, s):
        return jax.random.normal(k, shape, jnp.float32) * s

    return {
        'x': nrm(ks[0], (BATCH, SEQ, D), 1.0),
        'c': nrm(ks[1], (BATCH, D), 1.0),
        'ctx': nrm(ks[2], (BATCH, CTX_LEN, D), 1.0),
        'c_ctx': nrm(ks[3], (D,), 1.0),
        'ada_w': nrm(ks[4], (DEPTH, D, 6 * D), 0.5 * D ** -0.5),
        'ada_b': nrm(ks[5], (DEPTH, 6 * D), 0.02),
        'norm1_g': 1.0 + nrm(ks[6], (DEPTH, D), 0.02),
        'norm2_g': 1.0 + nrm(ks[7], (DEPTH, D), 0.02),
        'ffn_w_gate': nrm(ks[8], (DEPTH, D, F), D ** -0.5),
        'ffn_w_up': nrm(ks[9], (DEPTH, D, F), D ** -0.5),
        'ffn_w_down': nrm(ks[10], (DEPTH, F, D), F ** -0.5),
        'na_w_qkv': nrm(ks[11], (N_A, D, 3 * D), D ** -0.5),
        'na_w_o': nrm(ks[12], (N_A, D, D), D ** -0.5),
        'na_q_norm': 1.0 + nrm(ks[13], (N_A, hd), 0.02),
        'na_k_norm': 1.0 + nrm(ks[14], (N_A, hd), 0.02),
        'na_rpb': nrm(ks[15], (N_A, H, 2 * NA_WIN_ROWS - 1, 2 * NA_WIN_COLS - 1), 0.1),
        'gm_w_in': nrm(ks[16], (N_B, D, 2 * GMLP_WIDTH), D ** -0.5),
        'gm_b_in': nrm(ks[17], (N_B, 2 * GMLP_WIDTH), 0.02),
        'gm_ln_g': 1.0 + nrm(ks[18], (N_B, GMLP_WIDTH), 0.02),
        'gm_ln_b': nrm(ks[19], (N_B, GMLP_WIDTH), 0.02),
        'gm_w_s': nrm(ks[20], (N_B, GMLP_GROUPS, GMLP_CHUNK, GMLP_CHUNK), GMLP_CHUNK ** -0.5),
        'gm_b_s': 1.0 + nrm(ks[21], (N_B, GMLP_GROUPS, GMLP_CHUNK), 0.02),
        'gm_w_out': nrm(ks[22], (N_B, GMLP_WIDTH, D), GMLP_WIDTH ** -0.5),
        'pool_w': nrm(ks[23], (N_C, len(POOL_WINDOWS), POOL_GROUP, POOL_GROUP), POOL_GROUP ** -0.5),
        'pool_scale': 1.0 + nrm(ks[24], (N_C, D), 0.02),
    }


def reference(x, c, ctx, c_ctx, ada_w, ada_b, norm1_g, norm2_g, ffn_w_gate, ffn_w_up, ffn_w_down,
              na_w_qkv, na_w_o, na_q_norm, na_k_norm, na_rpb,
              gm_w_in, gm_b_in, gm_ln_g, gm_ln_b, gm_w_s, gm_b_s, gm_w_out,
              pool_w, pool_scale):
    s_lat = jax.nn.silu(c)
    s_ctx = jax.nn.silu(c_ctx)
    xc = ctx
    for i in range(DEPTH):
        kind, j = i % N_MIXERS, i // N_MIXERS
        last = i == DEPTH - 1
        need_ctx = (not last) or kind == 0
        sh1, sc1, g1, sh2, sc2, g2 = jnp.split((s_lat @ ada_w[i] + ada_b[i])[:, None, :], 6, axis=-1)
        csh1, csc1, cg1, csh2, csc2, cg2 = jnp.split(s_ctx @ ada_w[i] + ada_b[i], 6, axis=-1)

        h = rmsnorm(x, norm1_g[i]) * (1.0 + sc1) + sh1
        hc = rmsnorm(xc, norm1_g[i]) * (1.0 + csc1) + csh1 if need_ctx else None
        if kind == 0:
            y, yc = na_mixer(h, hc, na_w_qkv[j], na_w_o[j], na_q_norm[j], na_k_norm[j], na_rpb[j],
                             ctx_out=not last)
        elif kind == 1:
            gp = (gm_w_in[j], gm_b_in[j], gm_ln_g[j], gm_ln_b[j], gm_w_s[j], gm_b_s[j], gm_w_out[j])
            y = gmlp_mixer(h, *gp)
            yc = gmlp_mixer(hc, *gp) if not last else None
        else:
            y = pool_mixer(h, pool_w[j], pool_scale[j])
            yc = pool_mixer(hc, pool_w[j], pool_scale[j]) if not last else None

        x = x + g1 * y
        h2 = rmsnorm(x, norm2_g[i]) * (1.0 + sc2) + sh2
        x = x + g2 * swiglu(h2, ffn_w_gate[i], ffn_w_up[i], ffn_w_down[i])
        if not last:
            xc = xc + cg1 * yc
            hc2 = rmsnorm(xc, norm2_g[i]) * (1.0 + csc2) + csh2
            xc = xc + cg2 * swiglu(hc2, ffn_w_gate[i], ffn_w_up[i], ffn_w_down[i])
    return x
```

```python
import contextlib
import types
import numpy as np
import concourse.bass as bass
import concourse.mybir as mybir
from concourse.bass_utils import run_bass_kernel_spmd

F32 = mybir.dt.float32
BF16 = mybir.dt.bfloat16
AF = mybir.ActivationFunctionType
ALU = mybir.AluOpType

EPOCH = 12000
N_DMA_SEMS = 30
DMA_POOLS = {"sp": (0, 18), "act": (18, 8), "pool": (26, 4)}


class Buf:
    __slots__ = ("w", "r", "name")

    def __init__(self, name=""):
        self.w = None
        self.r = {}
        self.name = name


def _tok_key_val(tok):
    if tok[0] == "op":
        return tok[1].eng, tok[1].idx
    return ("dma", tok[1]), tok[2]


def inherit(new_bufs, old_bufs):
    merged = {}
    for b in old_bufs:
        toks = list(b.r.values())
        if b.w is not None:
            toks.append(b.w)
        for t in toks:
            k, v = _tok_key_val(t)
            if k not in merged or _tok_key_val(merged[k])[1] < v:
                merged[k] = t
    for nb in new_bufs:
        for k, t in merged.items():
            kk = ("inh", k)
            nb.r[kk] = t


class Op:
    __slots__ = ("eng", "fn", "waits", "idx", "need_inc", "dma_sem", "dma_val")


def _snapshot(fn):
    if fn.__closure__ is None:
        return fn
    cells = []
    for c in fn.__closure__:
        try:
            cells.append(types.CellType(c.cell_contents))
        except ValueError:
            cells.append(c)
    return types.FunctionType(fn.__code__, fn.__globals__, fn.__name__, fn.__defaults__, tuple(cells))


class Sched:
    ENGS = ("pe", "act", "dve", "pool", "sp")

    def __init__(self, nc):
        self.nc = nc
        self.q = {e: [] for e in self.ENGS}
        self.seen = {e: {} for e in self.ENGS}
        self.dma_vals = [0] * N_DMA_SEMS
        self.q_rr = {k: 0 for k in DMA_POOLS}

    def _add_wait(self, op, tok, waits):
        if tok is None:
            return
        if tok[0] == "op":
            src = tok[1]
            if src.eng == "pe" and op.eng == "pe":
                return
        key, val = _tok_key_val(tok)
        if self.seen[op.eng].get(key, -1) >= val:
            return
        cur = waits.get(key)
        if cur is None or cur[0] < val:
            waits[key] = (val, tok)

    def _mk(self, eng, fn, reads, writes, dma):
        op = Op()
        op.eng = eng
        op.fn = _snapshot(fn)
        op.need_inc = False
        op.idx = len(self.q[eng])
        op.dma_sem = None
        waits = {}
        for b in reads:
            self._add_wait(op, b.w, waits)
        for b in writes:
            self._add_wait(op, b.w, waits)
            for t in b.r.values():
                self._add_wait(op, t, waits)
        if dma:
            lo_, n_ = DMA_POOLS[eng]
            s = lo_ + self.q_rr[eng]
            self.q_rr[eng] = (self.q_rr[eng] + 1) % n_
            prev = self.dma_vals[s]
            if prev > 0:
                self._add_wait(op, ("dma", s, prev), waits)
            self.dma_vals[s] = prev + 16
            op.dma_sem = s
            op.dma_val = prev + 16
            tok = ("dma", s, prev + 16)
        else:
            tok = ("op", op)
        op.waits = []
        seen = self.seen[eng]
        for key, (val, t) in waits.items():
            seen[key] = val
            op.waits.append(t)
            if t[0] == "op":
                t[1].need_inc = True
        rkey = eng if not dma else ("dma", op.dma_sem)
        for b in reads:
            b.r[rkey] = tok
        for b in writes:
            b.w = tok
            b.r = {}
        self.q[eng].append(op)
        return op

    def op(self, eng, fn, reads=(), writes=()):
        return self._mk(eng, fn, reads, writes, False)

    def dma(self, eng, fn, reads=(), writes=()):
        return self._mk(eng, fn, reads, writes, True)

    def final_wait(self, eng, bufs):
        self._mk(eng, lambda e: e.nop(), list(bufs), list(bufs), False)

    def dma_fence(self, eng):
        fb = Buf("fence")
        for s in range(N_DMA_SEMS):
            if self.dma_vals[s] > 0:
                fb.r[("dma", s)] = ("dma", s, self.dma_vals[s])
        self._mk(eng, lambda e: e.nop(), [], [fb], False)

    def emit(self):
        nc = self.nc
        counts = {}
        n_epochs = {}
        for e in self.ENGS:
            c = 0
            for op in self.q[e]:
                if op.need_inc:
                    c += 1
                    counts[id(op)] = c
            n_epochs[e] = max(1, -(-c // EPOCH))
        with contextlib.ExitStack() as st:
            esems = {e: [st.enter_context(nc.semaphore(f"s_{e}_{i}")) for i in range(n_epochs[e])]
                     for e in self.ENGS}
            dsems = [st.enter_context(nc.semaphore(f"s_dma_{i}")) for i in range(N_DMA_SEMS)]
            block = st.enter_context(nc.Block())

            def resolve(tok):
                if tok[0] == "op":
                    c = counts[id(tok[1])]
                    ep = (c - 1) // EPOCH
                    return esems[tok[1].eng][ep], c - ep * EPOCH
                return dsems[tok[1]], tok[2]

            def run(ename):
                def body(e):
                    for op in self.q[ename]:
                        for t in op.waits:
                            s, v = resolve(t)
                            e.wait_ge(s, v)
                        ins = op.fn(e)
                        if op.dma_sem is not None:
                            ins.then_inc(dsems[op.dma_sem], 16)
                        elif op.need_inc:
                            c = counts[id(op)]
                            ep = (c - 1) // EPOCH
                            ins.then_inc(esems[ename][ep], 1)
                return body

            block.tensor(run("pe"))
            block.scalar(run("act"))
            block.vector(run("dve"))
            block.gpsimd(run("pool"))
            block.sync(run("sp"))
        return {e: len(self.q[e]) for e in self.ENGS}


D = 1024
DC = 8
FH = 2816
FC = 22
SL = 2048
SCX = 256
T = SL + SCX
DEPTH = 4
NSEQ = 2
H = 16
HD = 64
EPS = 1e-6
SUBS = [(0, 512), (512, 512), (1024, 512), (1536, 512), (2048, 256)]
SUPERS = [[0, 1], [2, 3, 4]]
GW_FFN = 256
NG_FFN = FH // GW_FFN

CFG = {"layers": [0, 1, 2, 3], "mixers": {"na", "gmlp", "pool"}, "ffn": True, "nseq": NSEQ}


class Arena:
    def __init__(self, tensor, nelem):
        self.t = tensor
        self.n = nelem
        self.off = 0
        self.marks = []

    def bf(self, shape):
        n = int(np.prod(shape))
        n = (n + 1) // 2 * 2
        assert self.off + n <= self.n, ("arena overflow", self.off, n, self.n)
        v = self.t[:, self.off:self.off + int(np.prod(shape))]
        self.off += n
        return _shape(v, shape)

    def f32(self, shape):
        n = int(np.prod(shape)) * 2
        assert self.off + n <= self.n, ("arena overflow", self.off, n, self.n)
        v = self.t[:, self.off:self.off + n].bitcast(F32)
        self.off += n
        return _shape(v, shape)

    def push(self):
        self.marks.append(self.off)

    def pop(self):
        self.off = self.marks.pop()


def _shape(v, shape):
    if len(shape) == 1:
        return v
    if len(shape) == 2:
        return v.rearrange("p (a b) -> p a b", a=shape[0])
    if len(shape) == 3:
        return v.rearrange("p (a b c) -> p a b c", a=shape[0], b=shape[1])
    raise ValueError(shape)


def build_nc(cfg=CFG):
    nc = bass.Bass("TRN2", target_bir_lowering=False)
    nseq = cfg["nseq"]

    def din(name, shape):
        return nc.dram_tensor(name, list(shape), F32, kind="ExternalInput").ap()

    x = din("x", [NSEQ, SL, D])
    c_in = din("c", [NSEQ, D])
    ctx = din("ctx", [NSEQ, SCX, D])
    c_ctx = din("c_ctx", [D])
    ada_w = din("ada_w", [DEPTH, D, 6 * D])
    ada_b = din("ada_b", [DEPTH, 6 * D])
    norm1_g = din("norm1_g", [DEPTH, D])
    norm2_g = din("norm2_g", [DEPTH, D])
    ffn_w_gate = din("ffn_w_gate", [DEPTH, D, FH])
    ffn_w_up = din("ffn_w_up", [DEPTH, D, FH])
    ffn_w_down = din("ffn_w_down", [DEPTH, FH, D])
    na_w_qkv = din("na_w_qkv", [2, D, 3 * D])
    na_w_o = din("na_w_o", [2, D, D])
    na_q_norm = din("na_q_norm", [2, HD])
    na_k_norm = din("na_k_norm", [2, HD])
    na_bias = din("na_bias", [2, H, 128, NB_TAB * 64])
    na_mask = din("na_mask", [128, NB_TAB * 64])
    gm_w_in = din("gm_w_in", [1, D, 2 * D])
    gm_b_in = din("gm_b_in", [1, 2 * D])
    gm_ln_g = din("gm_ln_g", [1, D])
    gm_ln_b = din("gm_ln_b", [1, D])
    gm_w_s = din("gm_w_s", [1, 8, 128, 128])
    gm_b_s = din("gm_b_s", [1, 8, 128])
    gm_w_out = din("gm_w_out", [1, D, D])
    pool_w = din("pool_w", [1, 4, 256, 256])
    pool_scale = din("pool_scale", [1, D])
    out = nc.dram_tensor("out", [NSEQ, SL, D], F32, kind="ExternalOutput").ap()

    def scratch(name, shape):
        return nc.dram_tensor(name, list(shape), BF16, kind="Internal").ap()

    layers = cfg["layers"]
    scr_g = {l: scratch(f"scr_g{l}", [128, NG_FFN, DC, GW_FFN]) for l in layers}
    scr_u = {l: scratch(f"scr_u{l}", [128, NG_FFN, DC, GW_FFN]) for l in layers}
    scr_d = {l: scratch(f"scr_d{l}", [128, DC, FC, 128]) for l in layers}
    scr_qkv = {j: scratch(f"scr_qkv{j}", [128, 6, DC, 512]) for j in range(2)}
    scr_wo = {j: scratch(f"scr_wo{j}", [128, 2, DC, 512]) for j in range(2)}
    scr_gin = scratch("scr_gin", [128, 4, DC, 512])
    scr_gout = scratch("scr_gout", [128, 2, DC, 512])

    with contextlib.ExitStack() as st:
        S = Sched(nc)
        AR_N = 64000
        arena_t = st.enter_context(nc.sbuf_tensor("arena", [128, AR_N], BF16))
        AR = Arena(arena_t, AR_N)
        XT = st.enter_context(nc.sbuf_tensor("XT", [128, DC, T], F32))
        ident = st.enter_context(nc.sbuf_tensor("ident", [128, 128], F32))
        onesd = st.enter_context(nc.sbuf_tensor("onesd", [128, 128], BF16))
        ones64 = st.enter_context(nc.sbuf_tensor("ones64", [128, 128], BF16))
        onesb = st.enter_context(nc.sbuf_tensor("onesb", [128, 128], BF16))
        onesf = st.enter_context(nc.sbuf_tensor("onesf", [128, 128], F32))
        MOD = st.enter_context(nc.sbuf_tensor("MOD", [128, DEPTH, 48, 4], F32))
        NG = st.enter_context(nc.sbuf_tensor("NG", [128, 2, DEPTH, DC], F32))
        A12 = st.enter_context(nc.sbuf_tensor("A12", [128, 2, DEPTH, DC, 4], F32))
        SMALL = st.enter_context(nc.sbuf_tensor("SMALL", [128, 80], F32))
        BVRT = st.enter_context(nc.sbuf_tensor("BVRT", [1, D], BF16))
        B_bvr = Buf("bvr")
        B_small2 = Buf("small2")
        B_small3 = Buf("small3")
        EPSC = st.enter_context(nc.sbuf_tensor("EPSC", [128, 2], F32))
        PSB = [st.enter_context(nc.psum_tensor(f"ps{i}", [128, 512], F32)) for i in range(8)]
        PSBUF = [Buf(f"ps{i}") for i in range(8)]
        ps_rr = [0]
        ps_pool = [list(range(8))]

        def psum():
            pool_ = ps_pool[0]
            i = pool_[ps_rr[0] % len(pool_)]
            ps_rr[0] += 1
            return PSB[i], PSBUF[i]

        B_XT = [[Buf(f"xt{c}_{s}") for s in range(len(SUBS))] for c in range(DC)]
        B_const = Buf("const")
        B_mod = Buf("mod")
        B_out = Buf("out")
        B_arena_all = []
        phase_old = []

        def new_phase():
            nonlocal B_arena_all, phase_old
            phase_old = B_arena_all
            B_arena_all = []
            AR.off = 0

        def abuf(name):
            b = Buf(name)
            inherit([b], phase_old)
            B_arena_all.append(b)
            return b

        S.op("pool", lambda e: e.memset(ident[:], 0.0), [], [B_const])
        S.op("pool", lambda e: e.affine_select(out=ident[:], in_=ident[:], pattern=[[-1, 128]],
                                               compare_op=ALU.not_equal, fill=1.0, base=0,
                                               channel_multiplier=1), [], [B_const])
        S.op("pool", lambda e: e.memset(EPSC[:], EPS), [], [B_const])
        S.op("pool", lambda e: e.memset(onesd[:], 1.0 / D), [], [B_const])
        S.op("pool", lambda e: e.memset(onesb[:], 1.0), [], [B_const])
        S.op("pool", lambda e: e.memset(onesf[:], 1.0), [], [B_const])
        S.op("pool", lambda e: e.memset(ones64[:], 0.0), [], [B_const])
        S.op("pool", lambda e: e.memset(ones64[0:64, 0:64], 1.0 / HD), [], [B_const])
        S.op("pool", lambda e: e.memset(ones64[64:128, 64:128], 1.0 / HD), [], [B_const])

        cast_rr = [0]

        pc_tiles = []

        def precast(src2d, K, N, dst, gw):
            for k in range(K):
                pc_tiles.append((src2d, k, N, dst, gw))

        def precast_emit(slots, depth=3):
            def load(i):
                src2d, k, N, dst, gw = pc_tiles[i]
                st32, stb, b32, bb = slots[i % len(slots)]
                S.dma("sp", lambda e: e.dma_start(out=st32[:, 0:N], in_=src2d[k * 128:(k + 1) * 128, :]), [], [b32])

            def cast_store(i):
                src2d, k, N, dst, gw = pc_tiles[i]
                st32, stb, b32, bb = slots[i % len(slots)]
                eng = ("dve", "pool")[i % 2]
                S.op(eng, lambda e: e.tensor_copy(out=stb[:, 0:N], in_=st32[:, 0:N]), [b32], [bb])
                S.dma("sp", lambda e: e.dma_start(out=dst[:, :, k, :], in_=stb[:, 0:N].rearrange("p (g c) -> p g c", c=gw)),
                      [bb], [])

            n = len(pc_tiles)
            for i in range(n + depth):
                if i < n:
                    load(i)
                if i >= depth:
                    cast_store(i - depth)

        new_phase()
        B_scr = Buf("scr")
        pc_slots = []
        for i in range(4):
            st32 = AR.f32([3072])
            stb = AR.bf([3072])
            pc_slots.append((st32, stb, abuf(f"pc32_{i}"), abuf(f"pcb_{i}")))
        pc_end = AR.off
        if cfg["ffn"]:
            for l in layers:
                precast(ffn_w_gate[l], DC, FH, scr_g[l], GW_FFN)
                precast(ffn_w_up[l], DC, FH, scr_u[l], GW_FFN)
                precast(ffn_w_down[l], FC, D, scr_d[l], 128)
        if "na" in cfg["mixers"]:
            for j in range(2):
                if (j * 3) in layers:
                    precast(na_w_qkv[j], DC, 3 * D, scr_qkv[j], 512)
                    precast(na_w_o[j], DC, D, scr_wo[j], 512)
        if "gmlp" in cfg["mixers"] and 1 in layers:
            precast(gm_w_in[0], DC, 2 * D, scr_gin, 512)
            precast(gm_w_out[0], DC, D, scr_gout, 512)

        precast_emit(pc_slots)
        S.dma_fence("sp")
        phase_old = []
        AR.off = pc_end
        stage = AR.f32([128])
        b_stage = abuf("stage")
        sT = AR.f32([32])
        b_sT = abuf("sT")
        for v in range(4):
            if v < NSEQ:
                src = c_in[v].rearrange("(c p) -> c p", p=128)
            else:
                src = c_ctx.rearrange("(c p) -> c p", p=128)
            S.dma("act", lambda e, v=v, src=src: e.dma_start(out=stage[v * 8:(v + 1) * 8, :], in_=src), [], [b_stage])
        pt, pb = psum()
        S.op("pe", lambda e, pt=pt: e.transpose(out=pt[:, 0:32], in_=stage[0:32, :], identity=ident[0:32, 0:32]),
             [b_stage, B_const], [pb])
        S.op("act", lambda e, pt=pt: e.activation(out=sT[:, 0:32], in_=pt[:, 0:32], func=AF.Silu), [], [pb, b_sT])
        def load_vec_fm(srcs, dst_ap, nrows, dbuf=None, dq="sp", ev="dve"):
            dbuf = dbuf or B_mod
            stg = AR.f32([128])
            bs = abuf("vstg")
            r0 = 0
            for s_ap in srcs:
                r = s_ap.shape[0]
                S.dma(dq, lambda e, s_ap=s_ap, r0=r0, r=r, stg=stg: e.dma_start(out=stg[r0:r0 + r, :], in_=s_ap), [], [bs])
                r0 += r
            assert r0 == nrows
            pt, pb = psum()
            S.op("pe", lambda e, pt=pt, stg=stg: e.transpose(out=pt[:, 0:nrows], in_=stg[0:nrows, :], identity=ident[0:nrows, 0:nrows]),
                 [bs, B_const], [pb])
            if ev == "act":
                S.op("act", lambda e, pt=pt: e.copy(out=dst_ap, in_=pt[:, 0:nrows]), [], [pb, dbuf])
            else:
                S.op("dve", lambda e, pt=pt: e.tensor_copy(out=dst_ap, in_=pt[:, 0:nrows]), [], [pb, dbuf])

        load_vec_fm([norm1_g.rearrange("l (c p) -> (l c) p", p=128), norm2_g.rearrange("l (c p) -> (l c) p", p=128)],
                    NG[:].rearrange("p a l c -> p (a l c)"), 2 * DEPTH * DC, None, "act", "act")
        AW_N = 512
        aw_slots = []
        for i in range(2):
            aw_slots.append((AR.f32([DC, AW_N]), abuf(f"aw{i}")))
        ab_slots = [(AR.f32([AW_N]), abuf(f"ab{i}")) for i in range(2)]
        gi = 0
        for l in layers:
            pt, pb = psum()
            for g in range(6 * D // AW_N):
                awt, awb = aw_slots[gi % 2]
                ab_row, b_ab = ab_slots[gi % 2]
                gi += 1
                S.dma("act", lambda e, l=l, g=g, ab_row=ab_row: e.dma_start(
                    out=ab_row[0:1, :], in_=ada_b[l:l + 1, g * AW_N:(g + 1) * AW_N]), [], [b_ab])
                S.dma("act", lambda e, l=l, g=g, awt=awt: e.dma_start(
                    out=awt, in_=ada_w[l][:, g * AW_N:(g + 1) * AW_N].rearrange("(k p) n -> p k n", p=128)), [], [awb])
                for jj in range(AW_N // 128):
                    j = g * (AW_N // 128) + jj
                    col = (j % 48) * 4
                    for k in range(DC):
                        S.op("pe", lambda e, pt=pt, awt=awt, jj=jj, k=k, col=col: e.matmul(
                            pt[:, col:col + 4], lhsT=awt[:, k, jj * 128:(jj + 1) * 128], rhs=sT[:, k:32:8],
                            start=(k == 0), stop=False), [awb, b_sT], [pb])
                    S.op("pe", lambda e, pt=pt, j=j, col=col: e.matmul(
                        pt[:, col:col + 4], lhsT=ab_row[0:1, jj * 128:(jj + 1) * 128], rhs=onesf[0:1, 0:4],
                        start=False, stop=True), [b_ab, B_const], [pb])
            S.op("act", lambda e, pt=pt, l=l: e.copy(out=MOD[:, l].rearrange("p j v -> p (j v)"), in_=pt[:, 0:192]),
                 [], [pb, B_mod])
        for l in layers:
            for n, which in ((0, 1), (1, 4)):
                for v in range(3):
                    S.op("dve", lambda e, l=l, n=n, which=which, v=v: e.scalar_tensor_tensor(
                        out=A12[:, n, l, :, v], in0=MOD[:, l, which * 8:(which + 1) * 8, v], scalar=1.0,
                        in1=NG[:, n, l, :], op0=ALU.add, op1=ALU.mult), [B_mod], [B_mod])

        def modcol(l, which, c, v):
            return MOD[:, l, which * 8 + c, v:v + 1]

        def vcol(seq, si):
            return 3 if False else (2 if si == 4 else seq)

        def load_x(seq):
            new_phase()
            slots = [(AR.f32([D]), abuf(f"xs{i}")) for i in range(3)]
            for tt in range(T // 128):
                stg, bs = slots[tt % 3]
                if tt < SL // 128:
                    src = x[seq, tt * 128:(tt + 1) * 128, :]
                else:
                    src = ctx[seq, (tt - 16) * 128:(tt - 15) * 128, :]
                S.dma("sp", lambda e, stg=stg, src=src: e.dma_start(out=stg, in_=src), [], [bs])
                si = min(tt // 4, 4)
                for half in range(2):
                    pt, pb = psum()
                    for cc in range(4):
                        c = half * 4 + cc
                        S.op("pe", lambda e, pt=pt, cc=cc, c=c, stg=stg: e.transpose(
                            out=pt[:, cc * 128:(cc + 1) * 128], in_=stg[:, c * 128:(c + 1) * 128], identity=ident[:]),
                            [bs, B_const], [pb])
                    eng = "act" if half == 0 else "dve"
                    wr = [B_XT[half * 4 + cc][si] for cc in range(4)]
                    dst = XT[:, half * 4:half * 4 + 4, tt * 128:(tt + 1) * 128]
                    srcp = pt[:, :].rearrange("p (c t) -> p c t", c=4)
                    if eng == "act":
                        S.op("act", lambda e, dst=dst, srcp=srcp: e.copy(out=dst, in_=srcp), [], [pb] + wr)
                    else:
                        S.op("dve", lambda e, dst=dst, srcp=srcp: e.tensor_copy(out=dst, in_=srcp), [], [pb] + wr)

        def store_x(seq):
            new_phase()
            slots = [(AR.f32([D]), abuf(f"os{i}")) for i in range(3)]
            for tt in range(SL // 128):
                stg, bs = slots[tt % 3]
                si = tt // 4
                for half in range(2):
                    pt, pb = psum()
                    for cc in range(4):
                        c = half * 4 + cc
                        S.op("pe", lambda e, pt=pt, cc=cc, c=c, tt=tt: e.transpose(
                            out=pt[:, cc * 128:(cc + 1) * 128], in_=XT[:, c, tt * 128:(tt + 1) * 128], identity=ident[:]),
                            [B_XT[c][si], B_const], [pb])
                    eng = "act" if half == 0 else "dve"
                    dst = stg[:, half * 512:(half + 1) * 512]
                    if eng == "act":
                        S.op("act", lambda e, dst=dst, pt=pt: e.copy(out=dst, in_=pt[:, :]), [], [pb, bs])
                    else:
                        S.op("dve", lambda e, dst=dst, pt=pt: e.tensor_copy(out=dst, in_=pt[:, :]), [], [pb, bs])
                S.dma("sp", lambda e, stg=stg, seq=seq, tt=tt: e.dma_start(out=out[seq, tt * 128:(tt + 1) * 128, :], in_=stg),
                      [bs], [])

        def norm_sub(l, n, seq, si, HTv, b_ht, SQ, b_sq, RS, b_rs, TMPs, hoff):
            t0, nt = SUBS[si]
            v = 2 if si == 4 else seq
            S.op("act", lambda e: e.activation(out=SQ[:, :, 0:nt], in_=XT[:, :, t0:t0 + nt], func=AF.Square),
                 [B_XT[c][si] for c in range(DC)], [b_sq])
            pt, pb = psum()
            for c in range(DC):
                S.op("pe", lambda e, pt=pt, c=c: e.matmul(pt[:, 0:nt], lhsT=onesd[:], rhs=SQ[:, c, 0:nt],
                                                           start=(c == 0), stop=(c == DC - 1)), [b_sq, B_const], [pb])
            S.op("act", lambda e, pt=pt: e.activation(out=RS[:, 0:nt], in_=pt[:, 0:nt], func=AF.Ln, bias=EPSC[:, 0:1], scale=1.0),
                 [B_const], [pb, b_rs])
            S.op("act", lambda e: e.activation(out=RS[:, 0:nt], in_=RS[:, 0:nt], func=AF.Exp, scale=-0.5), [], [b_rs])
            shw = 0 if n == 0 else 3
            for c in range(DC):
                tmp, btmp = TMPs[c % len(TMPs)]
                if c % 2 == 0:
                    S.op("dve", lambda e: e.tensor_tensor(out=tmp[:, 0:nt], in0=XT[:, c, t0:t0 + nt], in1=RS[:, 0:nt],
                                                          op=ALU.mult), [B_XT[c][si], b_rs], [btmp])
                    S.op("dve", lambda e: e.tensor_scalar(
                        out=HTv[:, c, hoff:hoff + nt], in0=tmp[:, 0:nt], scalar1=A12[:, n, l, c, v:v + 1],
                        scalar2=modcol(l, shw, c, v), op0=ALU.mult, op1=ALU.add), [btmp, B_mod], [b_ht])
                else:
                    S.op("pool", lambda e: e.tensor_tensor(out=tmp[:, 0:nt], in0=XT[:, c, t0:t0 + nt], in1=RS[:, 0:nt],
                                                           op=ALU.mult), [B_XT[c][si], b_rs], [btmp])
                    S.op("act", lambda e: e.activation(
                        out=HTv[:, c, hoff:hoff + nt], in_=tmp[:, 0:nt], func=AF.Identity,
                        scale=A12[:, n, l, c, v:v + 1], bias=modcol(l, shw, c, v)), [btmp, B_mod], [b_ht])

        def ffn_layer(l, seq, last=False):
            new_phase()
            subs = [0, 1, 2, 3] if last else [0, 1, 2, 3, 4]
            HS = [(AR.bf([DC, 512]), abuf(f"hts{i}")) for i in range(2)]
            GT = AR.bf([FC, 512]); b_gt = [abuf(f"gt{f}") for f in range(FC)]
            WGU = [(AR.bf([DC, GW_FFN]), AR.bf([DC, GW_FFN]), abuf(f"wg{i}"), abuf(f"wu{i}")) for i in range(3)]
            WDs = [(AR.bf([FC, 128]), abuf(f"wd{i}")) for i in range(3)]
            SQ = AR.bf([DC, 512]); b_sq = abuf("sq")
            SG = [(AR.bf([512]), abuf(f"sg{i}")) for i in range(2)]
            RS = AR.f32([512]); b_rs = abuf("rs")
            TMPs = [(AR.f32([512]), abuf(f"tmp{i}")) for i in range(4)]
            gcount = 0
            dcount = 0
            sgc = 0

            def do_norm(idx):
                si = subs[idx]
                HTs, b_hs = HS[idx % 2]
                norm_sub(l, 1, seq, si, HTs, b_hs, SQ, b_sq, RS, b_rs, TMPs, 0)

            do_norm(0)
            for idx, si in enumerate(subs):
                HTs, b_hs = HS[idx % 2]
                t0, nt = SUBS[si]
                v = 2 if si == 4 else seq
                for g in range(NG_FFN):
                    wg, wu, bwg, bwu = WGU[gcount % 3]
                    gcount += 1
                    S.dma("sp", lambda e: e.dma_start(out=wg, in_=scr_g[l][:, g]), [], [bwg])
                    S.dma("sp", lambda e: e.dma_start(out=wu, in_=scr_u[l][:, g]), [], [bwu])
                    for ff in range(GW_FFN // 128):
                        f = g * (GW_FFN // 128) + ff
                        pg, pgb = psum()
                        pu, pub = psum()
                        for k in range(DC):
                            S.op("pe", lambda e: e.matmul(
                                pg[:, 0:nt], lhsT=wg[:, k, ff * 128:(ff + 1) * 128], rhs=HTs[:, k, 0:nt],
                                start=(k == 0), stop=(k == DC - 1)), [bwg, b_hs], [pgb])
                        for k in range(DC):
                            S.op("pe", lambda e: e.matmul(
                                pu[:, 0:nt], lhsT=wu[:, k, ff * 128:(ff + 1) * 128], rhs=HTs[:, k, 0:nt],
                                start=(k == 0), stop=(k == DC - 1)), [bwu, b_hs], [pub])
                        sg, bsg = SG[sgc % 2]
                        sgc += 1
                        S.op("act", lambda e: e.activation(out=sg[:, 0:nt], in_=pg[:, 0:nt], func=AF.Silu), [], [pgb, bsg])
                        S.op("dve", lambda e: e.tensor_tensor(out=GT[:, f, 0:nt], in0=pu[:, 0:nt], in1=sg[:, 0:nt], op=ALU.mult),
                             [bsg], [pub, b_gt[f]])
                if idx + 1 < len(subs):
                    do_norm(idx + 1)
                for dc in range(DC):
                    wd, bwd = WDs[dcount % 3]
                    dcount += 1
                    S.dma("sp", lambda e: e.dma_start(out=wd, in_=scr_d[l][:, dc]), [], [bwd])
                    pd_, pdb_ = psum()
                    for f in range(FC):
                        S.op("pe", lambda e: e.matmul(
                            pd_[:, 0:nt], lhsT=wd[:, f, :], rhs=GT[:, f, 0:nt], start=(f == 0), stop=(f == FC - 1)),
                            [bwd, b_gt[f]], [pdb_])
                    S.op("dve", lambda e: e.scalar_tensor_tensor(
                        out=XT[:, dc, t0:t0 + nt], in0=pd_[:, 0:nt], scalar=modcol(l, 5, dc, v), in1=XT[:, dc, t0:t0 + nt],
                        op0=ALU.mult, op1=ALU.add), [B_mod], [pdb_, B_XT[dc][si]])

        B_small = Buf("small")

        def norm_full(l, seq):
            HT = AR.bf([DC, T])
            b_ht = [abuf(f"ht{si}") for si in range(5)]
            mark = AR.off
            SQ = AR.bf([DC, 512]); b_sq = abuf("sq")
            RS = AR.f32([512]); b_rs = abuf("rs")
            TMPs = [(AR.f32([512]), abuf(f"tmp{i}")) for i in range(2)]
            for si in range(5):
                norm_sub(l, 0, seq, si, HT, b_ht[si], SQ, b_sq, RS, b_rs, TMPs, SUBS[si][0])
            scratch_bufs = [b_sq, b_rs] + [t[1] for t in TMPs]
            AR.off = mark
            phase_old.extend(scratch_bufs)
            return HT, b_ht, scratch_bufs

        def abuf2(name, olds):
            b = abuf(name)
            inherit([b], olds)
            return b

        def pool_layer(l, seq, last):
            new_phase()
            HT, b_ht, olds = norm_full(l, seq)
            PW = AR.bf([4, 2, 256]); b_pw = abuf2("pw", olds)
            S.dma("pool", lambda e: e.dma_start(out=PW, in_=pool_w[0].rearrange("g (kc p) n -> p g kc n", p=128)), [], [b_pw])
            load_vec_fm([pool_scale[0].rearrange("(c p) -> c p", p=128)], SMALL[:, 0:8], 8, B_small)
            for v in range(3):
                S.op("dve", lambda e, v=v: e.tensor_tensor(out=SMALL[:, 8 + v * 8:16 + v * 8], in0=SMALL[:, 0:8],
                                                           in1=MOD[:, l, 16:24, v], op=ALU.mult), [B_mod, B_small], [B_small])
            PD = AR.bf([DC, T]); b_pd = [abuf2(f"pd{c}", olds) for c in range(DC)]
            ZS = {en: [(AR.f32([SL + 16]), abuf2(f"z{en}{i}", olds)) for i in range(2)] for en in ("dve", "pool")}
            for c in (6, 7, 4, 5, 2, 3, 0, 1):
                w = (2, 4, 8, 16)[c // 2]
                hw_ = w // 2
                peng = "pool" if c in (0, 2, 4, 6) else "dve"
                for (off, L) in ((0, SL), (SL, SCX)):
                    sis = [0, 1, 2, 3] if off == 0 else [4]
                    (za, bza), (zb, bzb) = ZS[peng]
                    S.op("pool", lambda e, za=za: e.memset(za[:, 0:8], 0.0), [], [bza])
                    S.op("pool", lambda e, za=za, L=L: e.memset(za[:, 8 + L:16 + L], 0.0), [], [bza])
                    S.op("act", lambda e, za=za, c=c, off=off, L=L: e.copy(out=za[:, 8:8 + L], in_=HT[:, c, off:off + L]),
                         [b_ht[si] for si in sis], [bza])
                    cur, bcur, oth, both = za, bza, zb, bzb
                    m = 1
                    while m < w:
                        n = L + 16 - m
                        S.op(peng, lambda e, cur=cur, oth=oth, m=m, n=n: e.tensor_tensor(
                            out=oth[:, 0:n], in0=cur[:, 0:n], in1=cur[:, m:m + n], op=ALU.add), [bcur], [both])
                        cur, bcur, oth, both = oth, both, cur, bcur
                        m *= 2
                    S.op("dve", lambda e, cur=cur, c=c, off=off, L=L, hw_=hw_, w=w: e.scalar_tensor_tensor(
                        out=PD[:, c, off:off + L], in0=cur[:, 8 - hw_:8 - hw_ + L], scalar=1.0 / w, in1=HT[:, c, off:off + L],
                        op0=ALU.mult, op1=ALU.subtract), [bcur] + [b_ht[si] for si in sis], [b_pd[c]])
                    edge = [(t, t + hw_) for t in range(hw_)] + [(t, L - t + hw_) for t in range(L - hw_ + 1, L)]
                    for (t, cnt) in edge:
                        S.op("dve", lambda e, cur=cur, c=c, off=off, t=t, cnt=cnt, hw_=hw_: e.scalar_tensor_tensor(
                            out=PD[:, c, off + t:off + t + 1], in0=cur[:, 8 - hw_ + t:9 - hw_ + t], scalar=1.0 / cnt,
                            in1=HT[:, c, off + t:off + t + 1], op0=ALU.mult, op1=ALU.subtract), [bcur], [b_pd[c]])
            for gi_ in range(4):
                for m in range(2):
                    oc = 2 * gi_ + m
                    for si in range(5):
                        t0, nt = SUBS[si]
                        v = 2 if si == 4 else seq
                        pt, pb = psum()
                        for kc in range(2):
                            S.op("pe", lambda e, pt=pt, gi_=gi_, kc=kc, m=m, t0=t0, nt=nt: e.matmul(
                                pt[:, 0:nt], lhsT=PW[:, gi_, kc, m * 128:(m + 1) * 128], rhs=PD[:, 2 * gi_ + kc, t0:t0 + nt],
                                start=(kc == 0), stop=(kc == 1)), [b_pw, b_pd[2 * gi_ + kc]], [pb])
                        S.op("dve", lambda e, pt=pt, oc=oc, t0=t0, nt=nt, v=v: e.scalar_tensor_tensor(
                            out=XT[:, oc, t0:t0 + nt], in0=pt[:, 0:nt], scalar=SMALL[:, 8 + v * 8 + oc:9 + v * 8 + oc],
                            in1=XT[:, oc, t0:t0 + nt], op0=ALU.mult, op1=ALU.add), [B_small], [pb, B_XT[oc][si]])

        def gmlp_layer(l, seq, last):
            new_phase()
            HT, b_ht, olds = norm_full(l, seq)
            WIN = AR.bf([4, DC, 512]); b_win = abuf2("win", olds)
            WOUT = AR.bf([2, DC, 512]); b_wout = abuf2("wout", olds)
            for g in range(4):
                S.dma("sp", lambda e, g=g: e.dma_start(out=WIN[:, g], in_=scr_gin[:, g]), [], [b_win])
            for g in range(2):
                S.dma("sp", lambda e, g=g: e.dma_start(out=WOUT[:, g], in_=scr_gout[:, g]), [], [b_wout])
            WSS = AR.f32([8, 128]); b_wss = abuf2("wss", olds)
            WST = AR.bf([8, 128]); b_wst = abuf2("wst", olds)
            S.dma("sp", lambda e: e.dma_start(out=WSS, in_=gm_w_s[0].rearrange("g p q -> p g q")), [], [b_wss])
            for hf in range(2):
                pt, pb = psum()
                for gg in range(4):
                    g = hf * 4 + gg
                    S.op("pe", lambda e, pt=pt, gg=gg, g=g: e.transpose(out=pt[:, gg * 128:(gg + 1) * 128], in_=WSS[:, g, :],
                                                                         identity=ident[:]), [b_wss, B_const], [pb])
                S.op("act", lambda e, pt=pt, hf=hf: e.copy(
                    out=WST[:, hf * 4:hf * 4 + 4, :], in_=pt[:, :].rearrange("p (g q) -> p g q", g=4)), [], [pb, b_wst])
            VG = AR.f32([D]); b_vg = abuf2("vg", olds)
            BSB = VG
            S.dma("sp", lambda e: e.dma_start(out=BSB, in_=gm_b_s[0].rearrange("g p -> (g p)").partition_broadcast(128)), [], [b_vg])
            load_vec_fm([gm_ln_b[0].rearrange("(c p) -> c p", p=128)], SMALL[:, 50:58], 8, B_small)
            CT = AR.f32([8, 128]); b_ct = abuf2("ct", olds)
            for hf in range(2):
                pt, pb = psum()
                S.op("pe", lambda e, pt=pt, hf=hf: e.matmul(
                    pt[:, :], lhsT=onesb[:, :], rhs=WST[:, hf * 4:hf * 4 + 4, :].rearrange("p g q -> p (g q)"),
                    start=True, stop=True), [b_wst, B_const], [pb])
                for gg in range(4):
                    g = hf * 4 + gg
                    S.op("dve", lambda e, pt=pt, gg=gg, g=g: e.scalar_tensor_tensor(
                        out=CT[:, g, :], in0=pt[:, gg * 128:(gg + 1) * 128], scalar=SMALL[:, 50 + g:51 + g],
                        in1=BSB[:, g * 128:(g + 1) * 128], op0=ALU.mult, op1=ALU.add), [B_small, b_vg], [pb, b_ct])
            LNG = AR.f32([D]); b_lng = abuf2("lng", olds)
            S.dma("sp", lambda e: e.dma_start(out=LNG, in_=gm_ln_g[0].partition_broadcast(128)), [], [b_lng])
            load_vec_fm([gm_b_in[0, 0:D].rearrange("(c p) -> c p", p=128)], SMALL[:, 32:40], 8, B_small)
            BVR = BVRT; b_bvr = B_bvr
            S.dma("pool", lambda e: e.dma_start(out=BVR[0:1, :], in_=gm_b_in[0:1, D:2 * D]), [], [b_bvr])
            UT = AR.bf([DC, 512]); b_ut = abuf2("ut", olds)
            GM = AR.bf([DC, 512]); b_gm = abuf2("gm", olds)
            VGs = [(VG, b_vg), (WSS.rearrange("p g q -> p (g q)"), b_wss)]
            VHs = [(AR.bf([D]), abuf2(f"vh{i}", olds)) for i in range(2)]
            TM = AR.f32([512]); b_tm = abuf2("tm", olds)
            STs = [SMALL[:, 40:48], SMALL[:, 64:72]]
            B_sts = [B_small3, B_small2]

            def u_proj(si):
                t0, nt = SUBS[si]
                for fc in range(DC):
                    pt, pb = psum()
                    for k in range(DC):
                        S.op("pe", lambda e: e.matmul(
                            pt[:, 0:nt], lhsT=WIN[:, fc // 4, k, (fc % 4) * 128:(fc % 4 + 1) * 128], rhs=HT[:, k, t0:t0 + nt],
                            start=(k == 0), stop=(k == DC - 1)), [b_win, b_ht[si]], [pb])
                    S.op("act", lambda e: e.activation(
                        out=UT[:, fc, 0:nt], in_=pt[:, 0:nt], func=AF.Gelu_apprx_tanh, bias=SMALL[:, 32 + fc:33 + fc], scale=1.0),
                        [B_small], [pb, b_ut])

            def stage1(ch):
                si, tc, slot = ch
                t0, nt = SUBS[si]
                tok0 = t0 + tc * 128
                VGc, b_vgc = VGs[slot]
                VHc, b_vhc = VHs[slot]
                ST = STs[slot]; bst = B_sts[slot]
                for hf in range(2):
                    pt, pb = psum()
                    for k in range(DC):
                        S.op("pe", lambda e: e.matmul(
                            pt[:, :], lhsT=HT[:, k, tok0:tok0 + 128], rhs=WIN[:, 2 + hf, k, :],
                            start=(k == 0), stop=False), [b_win, b_ht[si]], [pb])
                    S.op("pe", lambda e: e.matmul(
                        pt[:, :], lhsT=onesb[0:1, :], rhs=BVR[0:1, hf * 512:(hf + 1) * 512], start=False, stop=True),
                        [b_bvr, B_const], [pb])
                    S.op("act", lambda e: e.activation(
                        out=VGc[:, hf * 512:(hf + 1) * 512], in_=pt[:, :], func=AF.Gelu_apprx_tanh), [], [pb, b_vgc])
                S.op("act", lambda e: e.activation(out=VHc, in_=VGc, func=AF.Square), [b_vgc], [b_vhc])
                S.op("dve", lambda e: e.reduce_sum(out=ST[:, 0:1], in_=VGc, axis=mybir.AxisListType.X), [b_vgc], [bst])
                S.op("dve", lambda e: e.reduce_sum(out=ST[:, 2:3], in_=VHc, axis=mybir.AxisListType.X), [b_vhc], [bst])
                S.op("dve", lambda e: e.tensor_scalar(out=ST[:, 3:4], in0=ST[:, 0:1], scalar1=1.0 / D, scalar2=None,
                                                      op0=ALU.mult), [], [bst])
                S.op("dve", lambda e: e.tensor_tensor(out=ST[:, 4:5], in0=ST[:, 3:4], in1=ST[:, 3:4], op=ALU.mult), [], [bst])
                S.op("dve", lambda e: e.scalar_tensor_tensor(out=ST[:, 5:6], in0=ST[:, 2:3], scalar=1.0 / D, in1=ST[:, 4:5],
                                                             op0=ALU.mult, op1=ALU.subtract), [], [bst])
                S.op("act", lambda e: e.activation(out=ST[:, 6:7], in_=ST[:, 5:6], func=AF.Sqrt, bias=EPS, scale=1.0),
                     [], [bst])
                S.op("dve", lambda e: e.reciprocal(out=ST[:, 6:7], in_=ST[:, 6:7]), [], [bst])
                S.op("dve", lambda e: e.tensor_scalar(out=VGc, in0=VGc, scalar1=ST[:, 3:4], scalar2=ST[:, 6:7],
                                                      op0=ALU.subtract, op1=ALU.mult), [bst], [b_vgc])
                S.op("pool", lambda e: e.tensor_tensor(out=VHc, in0=VGc, in1=LNG, op=ALU.mult), [b_vgc, b_lng], [b_vhc])

            def stage2(ch):
                si, tc, slot = ch
                VHc, b_vhc = VHs[slot]
                for hf in range(2):
                    pt, pb = psum()
                    for gg in range(4):
                        g = hf * 4 + gg
                        S.op("pe", lambda e: e.matmul(
                            pt[:, gg * 128:(gg + 1) * 128], lhsT=VHc[:, g * 128:(g + 1) * 128], rhs=WST[:, g, :],
                            start=True, stop=True), [b_vhc, b_wst], [pb])
                    S.op("dve", lambda e: e.tensor_tensor(
                        out=TM, in0=pt[:, :], in1=CT[:, hf * 4:hf * 4 + 4, :].rearrange("p g q -> p (g q)"), op=ALU.add),
                        [b_ct], [pb, b_tm])
                    S.op("pool", lambda e: e.tensor_tensor(
                        out=GM[:, hf * 4:hf * 4 + 4, tc * 128:(tc + 1) * 128],
                        in0=TM.rearrange("p (g q) -> p g q", g=4),
                        in1=UT[:, hf * 4:hf * 4 + 4, tc * 128:(tc + 1) * 128], op=ALU.mult), [b_tm, b_ut], [b_gm])

            def out_proj(si):
                t0, nt = SUBS[si]
                v = 2 if si == 4 else seq
                for dc in range(DC):
                    pt, pb = psum()
                    for k in range(DC):
                        S.op("pe", lambda e: e.matmul(
                            pt[:, 0:nt], lhsT=WOUT[:, dc // 4, k, (dc % 4) * 128:(dc % 4 + 1) * 128], rhs=GM[:, k, 0:nt],
                            start=(k == 0), stop=(k == DC - 1)), [b_wout, b_gm], [pb])
                    S.op("dve", lambda e: e.scalar_tensor_tensor(
                        out=XT[:, dc, t0:t0 + nt], in0=pt[:, 0:nt], scalar=modcol(l, 2, dc, v), in1=XT[:, dc, t0:t0 + nt],
                        op0=ALU.mult, op1=ALU.add), [B_mod], [pb, B_XT[dc][si]])

            chunks = []
            for si in range(5):
                for tc in range(SUBS[si][1] // 128):
                    chunks.append((si, tc, len(chunks) % 2))
            u_proj(0)
            stage1(chunks[0])
            for i, ch in enumerate(chunks):
                if i + 1 < len(chunks):
                    stage1(chunks[i + 1])
                stage2(ch)
                si = ch[0]
                if i + 1 == len(chunks) or chunks[i + 1][0] != si:
                    out_proj(si)
                    if si + 1 < 5:
                        u_proj(si + 1)

        def na_jobs(qb):
            jobs = []
            for kt in range(16):
                rows = []
                for r in range(8 * qb, 8 * qb + 8):
                    rs = min(max(r - 4, 0), 24)
                    if 2 * kt + 1 >= rs and 2 * kt <= rs + 7:
                        interior = 4 <= r <= 28
                        idx = (3 - 2 * kt + r) if interior else (15 - 2 * kt + r)
                        rows.append((r, interior, idx))
                runs = []
                for (r, it, idx) in rows:
                    if runs and runs[-1][1] == r - 1 and runs[-1][3] == it:
                        runs[-1][1] = r
                    else:
                        runs.append([r, r, idx, it])
                if runs:
                    jobs.append((kt, [(a, b, i0) for (a, b, i0, _) in runs]))
            return jobs

        def na_layer(l, seq, last):
            j = l // 3
            new_phase()
            HT, b_ht, olds = norm_full(l, seq)
            nsub = 4 if last else 5
            for col, src_, mul in ((48, na_q_norm, HD ** -0.5), (49, na_k_norm, 1.0)):
                for hb in range(2):
                    S.dma("sp", lambda e, col=col, src_=src_, hb=hb: e.dma_start(
                        out=SMALL[hb * 64:(hb + 1) * 64, col:col + 1], in_=src_[j].rearrange("(p o) -> p o", o=1)), [], [B_small])
            S.op("dve", lambda e: e.tensor_scalar(out=SMALL[:, 48:49], in0=SMALL[:, 48:49], scalar1=HD ** -0.5, scalar2=None,
                                                  op0=ALU.mult), [], [B_small])
            WS = AR.bf([DC, 512]); b_ws = abuf2("ws", olds)
            QH = AR.bf([4, T]); b_qh = [[abuf2(f"qh{c}_{s_}", olds) for s_ in range(5)] for c in range(4)]
            KH = AR.bf([4, T]); b_kh = [abuf2(f"kh{c}", olds) for c in range(4)]
            VHf = AR.bf([18, 512]); b_vh = [abuf2(f"vh{t}", olds) for t in range(18)]
            NEG = AR.bf([NB_TAB * 64]); b_neg = abuf2("neg", olds)
            BT = [(AR.bf([NB_TAB * 64]), abuf2(f"bt{i}", olds)) for i in range(2)]
            ES = [(AR.bf([512]), abuf2(f"e{i}", olds)) for i in range(9)]
            SQ1 = AR.bf([512]); b_sq1 = abuf2("sq1", olds)
            RS1 = AR.f32([512]); b_rs1 = abuf2("rs1", olds)
            RD = AR.f32([512]); b_rd = abuf2("rd", olds)
            S.dma("pool", lambda e: e.dma_start(out=NEG, in_=na_mask), [], [b_neg])
            QZ = []
            for i in range(2):
                pair = []
                for ph in range(2):
                    qt = AR.bf([512]); qtb = abuf2(f"qz{i}{ph}", olds)
                    S.op("pool", lambda e, qt=qt: e.memset(qt, 0.0), [], [qtb])
                    pair.append((qt, qtb))
                QZ.append(pair)
            qzc = [0]
            ecnt = [0]
            scnt = [0]

            def headnorm(pt, pb, nt, gcol, dst, dbufs):
                S.op("dve", lambda e: e.tensor_copy(out=dst, in_=pt[:, 0:nt]), [], [pb] + dbufs)
                S.op("pool", lambda e: e.tensor_tensor(out=SQ1[:, 0:nt], in0=dst, in1=dst, op=ALU.mult), dbufs, [b_sq1])
                p2, p2b = psum()
                S.op("pe", lambda e: e.matmul(p2[:, 0:nt], lhsT=ones64[:], rhs=SQ1[:, 0:nt], start=True, stop=True),
                     [b_sq1, B_const], [p2b])
                S.op("act", lambda e: e.activation(out=RS1[:, 0:nt], in_=p2[:, 0:nt], func=AF.Ln, bias=EPSC[:, 0:1], scale=1.0),
                     [B_const], [p2b, b_rs1])
                S.op("act", lambda e: e.activation(out=RS1[:, 0:nt], in_=RS1[:, 0:nt], func=AF.Exp, scale=-0.5), [], [b_rs1])
                S.op("dve", lambda e: e.scalar_tensor_tensor(out=dst, in0=dst, scalar=SMALL[:, gcol:gcol + 1],
                                                             in1=RS1[:, 0:nt], op0=ALU.mult, op1=ALU.mult),
                     [b_rs1, B_small], dbufs)

            for hh in range(2):
                for (grp, kind) in ((hh, "q"), (2 + hh, "k"), (4 + hh, "v")):
                    S.dma("sp", lambda e, grp=grp: e.dma_start(out=WS, in_=scr_qkv[j][:, grp]), [], [b_ws])
                    if kind in ("q", "k"):
                        pitems = [(cc, si) for cc in range(4) for si in range(nsub if kind == "q" else 5)]
                        pend = []

                        def proj(cc, si):
                            t0, nt = SUBS[si]
                            pt, pb = psum()
                            for k in range(DC):
                                S.op("pe", lambda e: e.matmul(
                                    pt[:, 0:nt], lhsT=WS[:, k, cc * 128:(cc + 1) * 128], rhs=HT[:, k, t0:t0 + nt],
                                    start=(k == 0), stop=(k == DC - 1)), [b_ws, b_ht[si]], [pb])
                            return (cc, si, pt, pb)

                        def fin(item):
                            cc, si, pt, pb = item
                            t0, nt = SUBS[si]
                            if kind == "q":
                                headnorm(pt, pb, nt, 48, QH[:, cc, t0:t0 + nt], [b_qh[cc][si]])
                            else:
                                headnorm(pt, pb, nt, 49, KH[:, cc, t0:t0 + nt], [b_kh[cc]])

                        for idx_, (cc, si) in enumerate(pitems):
                            pend.append(proj(cc, si))
                            if len(pend) > 1:
                                fin(pend.pop(0))
                        while pend:
                            fin(pend.pop(0))
                    else:
                        for tt in range(18):
                            si = min(tt // 4, 4)
                            pt, pb = psum()
                            for k in range(DC):
                                S.op("pe", lambda e, pt=pt, k=k, tt=tt: e.matmul(
                                    pt[:, :], lhsT=HT[:, k, tt * 128:(tt + 1) * 128], rhs=WS[:, k, :],
                                    start=(k == 0), stop=(k == DC - 1)), [b_ws, b_ht[si]], [pb])
                            S.op("act", lambda e, pt=pt, tt=tt: e.copy(out=VHf[:, tt, :], in_=pt[:, :]), [], [pb, b_vh[tt]])
                ps_pool[0] = [0, 1, 2, 3]
                po = [PSB[4], PSB[5]]; pob = [PSBUF[4], PSBUF[5]]
                pd = [PSB[6], PSB[7]]; pdb = [PSBUF[6], PSBUF[7]]
                items = []
                for cc in range(4):
                    for qb in range(nsub):
                        q0, N = SUBS[qb]
                        blkd = {"cc": cc, "qb": qb, "q0": q0, "N": N, "newpair": qb == 0}
                        tiles = [(16, [(None, None, None)]), (17, [(None, None, None)])]
                        if qb < 4:
                            tiles += na_jobs(qb)
                        js = []
                        for (kt, runs) in tiles:
                            for (ra, rb, idx0) in runs:
                                if ra is None:
                                    c0, n = 0, N
                                else:
                                    c0, n = (ra - 8 * qb) * 64, 64 * (rb - ra + 1)
                                for ph in range(2):
                                    js.append({"blk": blkd, "kt": kt, "c0": c0, "n": n, "idx0": idx0, "ph": ph,
                                               "firstb": False, "lastb": False, "start": kt == 16})
                        js[0]["firstb"] = True
                        js[-1]["lastb"] = True
                        items += js

                def stage_a(it):
                    blkd = it["blk"]; cc = blkd["cc"]; qb = blkd["qb"]; q0 = blkd["q0"]; N = blkd["N"]
                    if it["firstb"]:
                        if blkd["newpair"]:
                            for ph in range(2):
                                h = hh * 8 + cc * 2 + ph
                                bt, bbt = BT[ph]
                                S.dma("pool", lambda e: e.dma_start(out=bt, in_=na_bias[j, h]), [], [bbt])
                                S.op("dve", lambda e: e.tensor_tensor(out=bt, in0=bt, in1=NEG, op=ALU.add), [b_neg], [bbt])
                                S.op("act", lambda e: e.activation(out=bt, in_=bt, func=AF.Exp), [], [bbt])
                        qz = QZ[qzc[0] % 2]
                        qzc[0] += 1
                        blkd["qz"] = qz
                        for ph in range(2):
                            pl = slice(ph * 64, (ph + 1) * 64)
                            qt, qtb = qz[ph]
                            S.op("pool", lambda e: e.tensor_copy(out=qt[pl, 0:N], in_=QH[pl, cc, q0:q0 + N]),
                                 [b_qh[cc][qb]], [qtb])
                    kt = it["kt"]; c0 = it["c0"]; n = it["n"]; idx0 = it["idx0"]; ph = it["ph"]
                    ktok = 2048 + (kt - 16) * 128 if kt >= 16 else kt * 128
                    qt, qtb = blkd["qz"][ph]
                    sp_, spb = psum()
                    S.op("pe", lambda e: e.matmul(sp_[:, 0:n], lhsT=KH[:, cc, ktok:ktok + 128], rhs=qt[:, c0:c0 + n],
                                                  start=True, stop=True), [b_kh[cc], qtb], [spb])
                    et, ebt = ES[ecnt[0] % len(ES)]
                    ecnt[0] += 1
                    it["et"] = (et, ebt)
                    S.op("act", lambda e: e.activation(out=et[:, 0:n], in_=sp_[:, 0:n], func=AF.Exp), [], [spb, ebt])
                    if idx0 is not None:
                        bt, bbt = BT[ph]
                        meng = "pool" if (scnt[0] % 3 == 2) else "dve"
                        scnt[0] += 1
                        S.op(meng, lambda e: e.tensor_tensor(out=et[:, 0:n], in0=et[:, 0:n], in1=bt[:, idx0 * 64:idx0 * 64 + n],
                                                             op=ALU.mult), [bbt], [ebt])

                def stage_b(it):
                    blkd = it["blk"]; cc = blkd["cc"]; qb = blkd["qb"]; q0 = blkd["q0"]; N = blkd["N"]
                    kt = it["kt"]; c0 = it["c0"]; n = it["n"]; ph = it["ph"]
                    et, ebt = it["et"]
                    f = it["start"]
                    pp = po[ph]; pq = pd[ph]
                    S.op("pe", lambda e: e.matmul(pp[:, c0:c0 + n], lhsT=VHf[:, kt, cc * 128:(cc + 1) * 128], rhs=et[:, 0:n],
                                                  start=f, stop=False, skip_group_check=True), [ebt, b_vh[kt]], [pob[ph]])
                    S.op("pe", lambda e: e.matmul(pq[:, c0:c0 + n], lhsT=onesb[:, :], rhs=et[:, 0:n],
                                                  start=f, stop=False, skip_group_check=True), [ebt, B_const], [pdb[ph]])
                    if it["lastb"]:
                        for ph2 in range(2):
                            pl = slice(ph2 * 64, (ph2 + 1) * 64)
                            pq2 = pd[ph2]
                            S.op("act", lambda e: e.activation(out=RD[pl, 0:N], in_=pq2[pl, 0:N], func=AF.Ln), [], [pdb[ph2], b_rd])
                        S.op("act", lambda e: e.activation(out=RD[:, 0:N], in_=RD[:, 0:N], func=AF.Exp, scale=-1.0), [], [b_rd])
                        for ph2 in range(2):
                            pl = slice(ph2 * 64, (ph2 + 1) * 64)
                            pp2 = po[ph2]
                            S.op("dve", lambda e: e.tensor_tensor(out=QH[pl, cc, q0:q0 + N], in0=pp2[pl, 0:N], in1=RD[pl, 0:N],
                                                                  op=ALU.mult), [b_rd], [pob[ph2], b_qh[cc][qb]])

                LOOK = 5
                for i in range(len(items) + LOOK):
                    if i < len(items):
                        stage_a(items[i])
                    if i >= LOOK:
                        stage_b(items[i - LOOK])
                ps_pool[0] = list(range(8))
                WO = WS.rearrange("p k n -> p (k n)").rearrange("p (g k n) -> p g k n", g=2, k=4)
                S.dma("sp", lambda e, hh=hh: e.dma_start(out=WO, in_=scr_wo[j][:, :, 4 * hh:4 * hh + 4, :]), [], [b_ws])
                for dc in range(DC):
                    for si in range(nsub):
                        t0, nt = SUBS[si]
                        v = 2 if si == 4 else seq
                        pt, pb = psum()
                        for cc in range(4):
                            S.op("pe", lambda e, pt=pt, dc=dc, cc=cc, t0=t0, nt=nt: e.matmul(
                                pt[:, 0:nt], lhsT=WO[:, dc // 4, cc, (dc % 4) * 128:(dc % 4 + 1) * 128], rhs=QH[:, cc, t0:t0 + nt],
                                start=(cc == 0), stop=(cc == 3)), [b_ws, b_qh[cc][si]], [pb])
                        S.op("dve", lambda e, pt=pt, dc=dc, t0=t0, nt=nt, v=v: e.scalar_tensor_tensor(
                            out=XT[:, dc, t0:t0 + nt], in0=pt[:, 0:nt], scalar=modcol(l, 2, dc, v), in1=XT[:, dc, t0:t0 + nt],
                            op0=ALU.mult, op1=ALU.add), [B_mod], [pb, B_XT[dc][si]])

        MIX = {'pool': pool_layer, 'gmlp': gmlp_layer, 'na': na_layer}

        for seq in range(nseq):
            load_x(seq)
            for l in layers:
                kind = l % 3
                last = (l == DEPTH - 1)
                if kind == 0 and "na" in cfg["mixers"]:
                    MIX["na"](l, seq, last)
                elif kind == 1 and "gmlp" in cfg["mixers"]:
                    MIX["gmlp"](l, seq, last)
                elif kind == 2 and "pool" in cfg["mixers"]:
                    MIX["pool"](l, seq, last)
                if cfg["ffn"]:
                    ffn_layer(l, seq, last)
            store_x(seq)
        S.dma_fence("sp")
        stats = S.emit()
    return nc, stats


NB_TAB = 23
MIXER_BUILDERS = {}


def _prep_inputs(inputs, core):
    b0 = core * NSEQ
    m = {}
    m["x"] = np.ascontiguousarray(inputs["x"][b0:b0 + NSEQ])
    m["c"] = np.ascontiguousarray(inputs["c"][b0:b0 + NSEQ])
    m["ctx"] = np.ascontiguousarray(inputs["ctx"][b0:b0 + NSEQ])
    for k in ("c_ctx", "ada_w", "ada_b", "norm1_g", "norm2_g", "ffn_w_gate", "ffn_w_up", "ffn_w_down",
              "na_w_qkv", "na_w_o", "na_q_norm", "na_k_norm", "gm_w_in", "gm_b_in", "gm_ln_g", "gm_ln_b",
              "gm_w_s", "gm_b_s", "gm_w_out", "pool_w", "pool_scale"):
        m[k] = np.ascontiguousarray(inputs[k], dtype=np.float32)
    return m


def kernel(**inputs):
    inputs = {k: np.asarray(v) for k, v in inputs.items()}
    nc, _ = build_nc(CFG)
    extra = na_tables(inputs["na_rpb"])
    in_maps = []
    for core in range(8):
        m = _prep_inputs(inputs, core)
        m.update(extra)
        in_maps.append(m)
    res = run_bass_kernel_spmd(nc, in_maps, core_ids=list(range(8)))
    return np.concatenate([r["out"] for r in res.results], axis=0).astype(np.float32)


def na_tables(rpb):
    rpb = np.asarray(rpb, dtype=np.float32)
    tab = np.zeros((2, H, 128, NB_TAB, 64), np.float32)
    neg = np.full((128, NB_TAB, 64), -30000.0, np.float32)
    qcol = np.arange(64)
    cstart = np.clip(qcol - 8, 0, 48)
    kcol = np.arange(64)
    ok_col = (kcol[:, None] >= cstart[None, :]) & (kcol[:, None] < cstart[None, :] + 16)
    dcm = np.clip(kcol[:, None] - qcol[None, :] + 15, 0, 30)
    for idx in range(NB_TAB):
        if idx < 9:
            delta, lo, hi = 3 - idx, -4, 3
        else:
            delta, lo, hi = 6 - (idx - 9), -7, 7
        for krl in range(2):
            dr = delta + krl
            if not (lo <= dr <= hi):
                continue
            g = rpb[:, :, dr + 7, :][:, :, dcm]
            sel = np.broadcast_to(ok_col, g.shape)
            blk = tab[:, :, krl * 64:(krl + 1) * 64, idx, :]
            blk[sel] = g[sel]
            neg[krl * 64:(krl + 1) * 64, idx, :][ok_col] = 0.0
    return {"na_bias": np.ascontiguousarray(tab.reshape(2, H, 128, NB_TAB * 64)),
            "na_mask": np.ascontiguousarray(neg.reshape(128, NB_TAB * 64))}
```

```python
import contextlib
import types
import numpy as np
import concourse.bass as bass
import concourse.mybir as mybir
from concourse.bass_utils import run_bass_kernel_spmd

F32 = mybir.dt.float32
BF16 = mybir.dt.bfloat16
AF = mybir.ActivationFunctionType
ALU = mybir.AluOpType

EPOCH = 12000
N_DMA_SEMS = 30
DMA_POOLS = {"sp": (0, 18), "act": (18, 8), "pool": (26, 4)}


class Buf:
    __slots__ = ("w", "r", "name")

    def __init__(self, name=""):
        self.w = None
        self.r = {}
        self.name = name


def _tok_key_val(tok):
    if tok[0] == "op":
        return tok[1].eng, tok[1].idx
    return ("dma", tok[1]), tok[2]


def inherit(new_bufs, old_bufs):
    merged = {}
    for b in old_bufs:
        toks = list(b.r.values())
        if b.w is not None:
            toks.append(b.w)
        for t in toks:
            k, v = _tok_key_val(t)
            if k not in merged or _tok_key_val(merged[k])[1] < v:
                merged[k] = t
    for nb in new_bufs:
        for k, t in merged.items():
            kk = ("inh", k)
            nb.r[kk] = t


class Op:
    __slots__ = ("eng", "fn", "waits", "idx", "need_inc", "dma_sem", "dma_val")


def _snapshot(fn):
    if fn.__closure__ is None:
        return fn
    cells = []
    for c in fn.__closure__:
        try:
            cells.append(types.CellType(c.cell_contents))
        except ValueError:
            cells.append(c)
    return types.FunctionType(fn.__code__, fn.__globals__, fn.__name__, fn.__defaults__, tuple(cells))


class Sched:
    ENGS = ("pe", "act", "dve", "pool", "sp")

    def __init__(self, nc):
        self.nc = nc
        self.q = {e: [] for e in self.ENGS}
        self.seen = {e: {} for e in self.ENGS}
        self.dma_vals = [0] * N_DMA_SEMS
        self.q_rr = {k: 0 for k in DMA_POOLS}

    def _add_wait(self, op, tok, waits):
        if tok is None:
            return
        if tok[0] == "op":
            src = tok[1]
            if src.eng == "pe" and op.eng == "pe":
                return
        key, val = _tok_key_val(tok)
        if self.seen[op.eng].get(key, -1) >= val:
            return
        cur = waits.get(key)
        if cur is None or cur[0] < val:
            waits[key] = (val, tok)

    def _mk(self, eng, fn, reads, writes, dma):
        op = Op()
        op.eng = eng
        op.fn = _snapshot(fn)
        op.need_inc = False
        op.idx = len(self.q[eng])
        op.dma_sem = None
        waits = {}
        for b in reads:
            self._add_wait(op, b.w, waits)
        for b in writes:
            self._add_wait(op, b.w, waits)
            for t in b.r.values():
                self._add_wait(op, t, waits)
        if dma:
            lo_, n_ = DMA_POOLS[eng]
            s = lo_ + self.q_rr[eng]
            self.q_rr[eng] = (self.q_rr[eng] + 1) % n_
            prev = self.dma_vals[s]
            if prev > 0:
                self._add_wait(op, ("dma", s, prev), waits)
            self.dma_vals[s] = prev + 16
            op.dma_sem = s
            op.dma_val = prev + 16
            tok = ("dma", s, prev + 16)
        else:
            tok = ("op", op)
        op.waits = []
        seen = self.seen[eng]
        for key, (val, t) in waits.items():
            seen[key] = val
            op.waits.append(t)
            if t[0] == "op":
                t[1].need_inc = True
        rkey = eng if not dma else ("dma", op.dma_sem)
        for b in reads:
            b.r[rkey] = tok
        for b in writes:
            b.w = tok
            b.r = {}
        self.q[eng].append(op)
        return op

    def op(self, eng, fn, reads=(), writes=()):
        return self._mk(eng, fn, reads, writes, False)

    def dma(self, eng, fn, reads=(), writes=()):
        return self._mk(eng, fn, reads, writes, True)

    def final_wait(self, eng, bufs):
        self._mk(eng, lambda e: e.nop(), list(bufs), list(bufs), False)

    def dma_fence(self, eng):
        fb = Buf("fence")
        for s in range(N_DMA_SEMS):
            if self.dma_vals[s] > 0:
                fb.r[("dma", s)] = ("dma", s, self.dma_vals[s])
        self._mk(eng, lambda e: e.nop(), [], [fb], False)

    def emit(self):
        nc = self.nc
        counts = {}
        n_epochs = {}
        for e in self.ENGS:
            c = 0
            for op in self.q[e]:
                if op.need_inc:
                    c += 1
                    counts[id(op)] = c
            n_epochs[e] = max(1, -(-c // EPOCH))
        with contextlib.ExitStack() as st:
            esems = {e: [st.enter_context(nc.semaphore(f"s_{e}_{i}")) for i in range(n_epochs[e])]
                     for e in self.ENGS}
            dsems = [st.enter_context(nc.semaphore(f"s_dma_{i}")) for i in range(N_DMA_SEMS)]
            block = st.enter_context(nc.Block())

            def resolve(tok):
                if tok[0] == "op":
                    c = counts[id(tok[1])]
                    ep = (c - 1) // EPOCH
                    return esems[tok[1].eng][ep], c - ep * EPOCH
                return dsems[tok[1]], tok[2]

            def run(ename):
                def body(e):
                    for op in self.q[ename]:
                        for t in op.waits:
                            s, v = resolve(t)
                            e.wait_ge(s, v)
                        ins = op.fn(e)
                        if op.dma_sem is not None:
                            ins.then_inc(dsems[op.dma_sem], 16)
                        elif op.need_inc:
                            c = counts[id(op)]
                            ep = (c - 1) // EPOCH
                            ins.then_inc(esems[ename][ep], 1)
                return body

            block.tensor(run("pe"))
            block.scalar(run("act"))
            block.vector(run("dve"))
            block.gpsimd(run("pool"))
            block.sync(run("sp"))
        return {e: len(self.q[e]) for e in self.ENGS}


D = 1024
DC = 8
FH = 2816
FC = 22
SL = 2048
SCX = 256
T = SL + SCX
DEPTH = 4
NSEQ = 2
H = 16
HD = 64
EPS = 1e-6
SUBS = [(0, 512), (512, 512), (1024, 512), (1536, 512), (2048, 256)]
SUPERS = [[0, 1], [2, 3, 4]]
GW_FFN = 256
NG_FFN = FH // GW_FFN

CFG = {"layers": [0, 1, 2, 3], "mixers": {"na", "gmlp", "pool"}, "ffn": True, "nseq": NSEQ}


class Arena:
    def __init__(self, tensor, nelem):
        self.t = tensor
        self.n = nelem
        self.off = 0
        self.marks = []

    def bf(self, shape):
        n = int(np.prod(shape))
        n = (n + 1) // 2 * 2
        assert self.off + n <= self.n, ("arena overflow", self.off, n, self.n)
        v = self.t[:, self.off:self.off + int(np.prod(shape))]
        self.off += n
        return _shape(v, shape)

    def f32(self, shape):
        n = int(np.prod(shape)) * 2
        assert self.off + n <= self.n, ("arena overflow", self.off, n, self.n)
        v = self.t[:, self.off:self.off + n].bitcast(F32)
        self.off += n
        return _shape(v, shape)

    def push(self):
        self.marks.append(self.off)

    def pop(self):
        self.off = self.marks.pop()


def _shape(v, shape):
    if len(shape) == 1:
        return v
    if len(shape) == 2:
        return v.rearrange("p (a b) -> p a b", a=shape[0])
    if len(shape) == 3:
        return v.rearrange("p (a b c) -> p a b c", a=shape[0], b=shape[1])
    raise ValueError(shape)


def build_nc(cfg=CFG):
    nc = bass.Bass("TRN2", target_bir_lowering=False)
    nseq = cfg["nseq"]

    def din(name, shape):
        return nc.dram_tensor(name, list(shape), F32, kind="ExternalInput").ap()

    x = din("x", [NSEQ, SL, D])
    c_in = din("c", [NSEQ, D])
    ctx = din("ctx", [NSEQ, SCX, D])
    c_ctx = din("c_ctx", [D])
    ada_w = din("ada_w", [DEPTH, D, 6 * D])
    ada_b = din("ada_b", [DEPTH, 6 * D])
    norm1_g = din("norm1_g", [DEPTH, D])
    norm2_g = din("norm2_g", [DEPTH, D])
    ffn_w_gate = din("ffn_w_gate", [DEPTH, D, FH])
    ffn_w_up = din("ffn_w_up", [DEPTH, D, FH])
    ffn_w_down = din("ffn_w_down", [DEPTH, FH, D])
    na_w_qkv = din("na_w_qkv", [2, D, 3 * D])
    na_w_o = din("na_w_o", [2, D, D])
    na_q_norm = din("na_q_norm", [2, HD])
    na_k_norm = din("na_k_norm", [2, HD])
    na_bias = din("na_bias", [2, H, 128, NB_TAB * 64])
    na_mask = din("na_mask", [128, NB_TAB * 64])
    gm_w_in = din("gm_w_in", [1, D, 2 * D])
    gm_b_in = din("gm_b_in", [1, 2 * D])
    gm_ln_g = din("gm_ln_g", [1, D])
    gm_ln_b = din("gm_ln_b", [1, D])
    gm_w_s = din("gm_w_s", [1, 8, 128, 128])
    gm_b_s = din("gm_b_s", [1, 8, 128])
    gm_w_out = din("gm_w_out", [1, D, D])
    pool_w = din("pool_w", [1, 4, 256, 256])
    pool_scale = din("pool_scale", [1, D])
    out = nc.dram_tensor("out", [NSEQ, SL, D], F32, kind="ExternalOutput").ap()

    def scratch(name, shape):
        return nc.dram_tensor(name, list(shape), BF16, kind="Internal").ap()

    layers = cfg["layers"]
    scr_g = {l: scratch(f"scr_g{l}", [128, NG_FFN, DC, GW_FFN]) for l in layers}
    scr_u = {l: scratch(f"scr_u{l}", [128, NG_FFN, DC, GW_FFN]) for l in layers}
    scr_d = {l: scratch(f"scr_d{l}", [128, DC, FC, 128]) for l in layers}
    scr_qkv = {j: scratch(f"scr_qkv{j}", [128, 6, DC, 512]) for j in range(2)}
    scr_wo = {j: scratch(f"scr_wo{j}", [128, 2, DC, 512]) for j in range(2)}
    scr_gin = scratch("scr_gin", [128, 4, DC, 512])
    scr_gout = scratch("scr_gout", [128, 2, DC, 512])

    with contextlib.ExitStack() as st:
        S = Sched(nc)
        AR_N = 64000
        arena_t = st.enter_context(nc.sbuf_tensor("arena", [128, AR_N], BF16))
        AR = Arena(arena_t, AR_N)
        XT = st.enter_context(nc.sbuf_tensor("XT", [128, DC, T], F32))
        ident = st.enter_context(nc.sbuf_tensor("ident", [128, 128], F32))
        onesd = st.enter_context(nc.sbuf_tensor("onesd", [128, 128], BF16))
        ones64 = st.enter_context(nc.sbuf_tensor("ones64", [128, 128], BF16))
        onesb = st.enter_context(nc.sbuf_tensor("onesb", [128, 128], BF16))
        onesf = st.enter_context(nc.sbuf_tensor("onesf", [128, 128], F32))
        MOD = st.enter_context(nc.sbuf_tensor("MOD", [128, DEPTH, 48, 4], F32))
        NG = st.enter_context(nc.sbuf_tensor("NG", [128, 2, DEPTH, DC], F32))
        A12 = st.enter_context(nc.sbuf_tensor("A12", [128, 2, DEPTH, DC, 4], F32))
        SMALL = st.enter_context(nc.sbuf_tensor("SMALL", [128, 80], F32))
        BVRT = st.enter_context(nc.sbuf_tensor("BVRT", [1, D], BF16))
        B_bvr = Buf("bvr")
        B_small2 = Buf("small2")
        B_small3 = Buf("small3")
        EPSC = st.enter_context(nc.sbuf_tensor("EPSC", [128, 2], F32))
        PSB = [st.enter_context(nc.psum_tensor(f"ps{i}", [128, 512], F32)) for i in range(8)]
        PSBUF = [Buf(f"ps{i}") for i in range(8)]
        ps_rr = [0]
        ps_pool = [list(range(8))]

        def psum():
            pool_ = ps_pool[0]
            i = pool_[ps_rr[0] % len(pool_)]
            ps_rr[0] += 1
            return PSB[i], PSBUF[i]

        B_XT = [[Buf(f"xt{c}_{s}") for s in range(len(SUBS))] for c in range(DC)]
        B_const = Buf("const")
        B_mod = Buf("mod")
        B_out = Buf("out")
        B_arena_all = []
        phase_old = []

        def new_phase():
            nonlocal B_arena_all, phase_old
            phase_old = B_arena_all
            B_arena_all = []
            AR.off = 0

        def abuf(name):
            b = Buf(name)
            inherit([b], phase_old)
            B_arena_all.append(b)
            return b

        S.op("pool", lambda e: e.memset(ident[:], 0.0), [], [B_const])
        S.op("pool", lambda e: e.affine_select(out=ident[:], in_=ident[:], pattern=[[-1, 128]],
                                               compare_op=ALU.not_equal, fill=1.0, base=0,
                                               channel_multiplier=1), [], [B_const])
        S.op("pool", lambda e: e.memset(EPSC[:], EPS), [], [B_const])
        S.op("pool", lambda e: e.memset(onesd[:], 1.0 / D), [], [B_const])
        S.op("pool", lambda e: e.memset(onesb[:], 1.0), [], [B_const])
        S.op("pool", lambda e: e.memset(onesf[:], 1.0), [], [B_const])
        S.op("pool", lambda e: e.memset(ones64[:], 0.0), [], [B_const])
        S.op("pool", lambda e: e.memset(ones64[0:64, 0:64], 1.0 / HD), [], [B_const])
        S.op("pool", lambda e: e.memset(ones64[64:128, 64:128], 1.0 / HD), [], [B_const])

        cast_rr = [0]

        pc_tiles = []

        def precast(src2d, K, N, dst, gw):
            for k in range(K):
                pc_tiles.append((src2d, k, N, dst, gw))

        def precast_emit(slots, depth=3):
            def load(i):
                src2d, k, N, dst, gw = pc_tiles[i]
                st32, stb, b32, bb = slots[i % len(slots)]
                S.dma("sp", lambda e: e.dma_start(out=st32[:, 0:N], in_=src2d[k * 128:(k + 1) * 128, :]), [], [b32])

            def cast_store(i):
                src2d, k, N, dst, gw = pc_tiles[i]
                st32, stb, b32, bb = slots[i % len(slots)]
                eng = ("dve", "pool")[i % 2]
                S.op(eng, lambda e: e.tensor_copy(out=stb[:, 0:N], in_=st32[:, 0:N]), [b32], [bb])
                S.dma("sp", lambda e: e.dma_start(out=dst[:, :, k, :], in_=stb[:, 0:N].rearrange("p (g c) -> p g c", c=gw)),
                      [bb], [])

            n = len(pc_tiles)
            for i in range(n + depth):
                if i < n:
                    load(i)
                if i >= depth:
                    cast_store(i - depth)

        new_phase()
        B_scr = Buf("scr")
        pc_slots = []
        for i in range(4):
            st32 = AR.f32([3072])
            stb = AR.bf([3072])
            pc_slots.append((st32, stb, abuf(f"pc32_{i}"), abuf(f"pcb_{i}")))
        pc_end = AR.off
        if cfg["ffn"]:
            for l in layers:
                precast(ffn_w_gate[l], DC, FH, scr_g[l], GW_FFN)
                precast(ffn_w_up[l], DC, FH, scr_u[l], GW_FFN)
                precast(ffn_w_down[l], FC, D, scr_d[l], 128)
        if "na" in cfg["mixers"]:
            for j in range(2):
                if (j * 3) in layers:
                    precast(na_w_qkv[j], DC, 3 * D, scr_qkv[j], 512)
                    precast(na_w_o[j], DC, D, scr_wo[j], 512)
        if "gmlp" in cfg["mixers"] and 1 in layers:
            precast(gm_w_in[0], DC, 2 * D, scr_gin, 512)
            precast(gm_w_out[0], DC, D, scr_gout, 512)

        precast_emit(pc_slots)
        S.dma_fence("sp")
        phase_old = []
        AR.off = pc_end
        stage = AR.f32([128])
        b_stage = abuf("stage")
        sT = AR.f32([32])
        b_sT = abuf("sT")
        for v in range(4):
            if v < NSEQ:
                src = c_in[v].rearrange("(c p) -> c p", p=128)
            else:
                src = c_ctx.rearrange("(c p) -> c p", p=128)
            S.dma("act", lambda e, v=v, src=src: e.dma_start(out=stage[v * 8:(v + 1) * 8, :], in_=src), [], [b_stage])
        pt, pb = psum()
        S.op("pe", lambda e, pt=pt: e.transpose(out=pt[:, 0:32], in_=stage[0:32, :], identity=ident[0:32, 0:32]),
             [b_stage, B_const], [pb])
        S.op("act", lambda e, pt=pt: e.activation(out=sT[:, 0:32], in_=pt[:, 0:32], func=AF.Silu), [], [pb, b_sT])
        def load_vec_fm(srcs, dst_ap, nrows, dbuf=None, dq="sp", ev="dve"):
            dbuf = dbuf or B_mod
            stg = AR.f32([128])
            bs = abuf("vstg")
            r0 = 0
            for s_ap in srcs:
                r = s_ap.shape[0]
                S.dma(dq, lambda e, s_ap=s_ap, r0=r0, r=r, stg=stg: e.dma_start(out=stg[r0:r0 + r, :], in_=s_ap), [], [bs])
                r0 += r
            assert r0 == nrows
            pt, pb = psum()
            S.op("pe", lambda e, pt=pt, stg=stg: e.transpose(out=pt[:, 0:nrows], in_=stg[0:nrows, :], identity=ident[0:nrows, 0:nrows]),
                 [bs, B_const], [pb])
            if ev == "act":
                S.op("act", lambda e, pt=pt: e.copy(out=dst_ap, in_=pt[:, 0:nrows]), [], [pb, dbuf])
            else:
                S.op("dve", lambda e, pt=pt: e.tensor_copy(out=dst_ap, in_=pt[:, 0:nrows]), [], [pb, dbuf])

        load_vec_fm([norm1_g.rearrange("l (c p) -> (l c) p", p=128), norm2_g.rearrange("l (c p) -> (l c) p", p=128)],
                    NG[:].rearrange("p a l c -> p (a l c)"), 2 * DEPTH * DC, None, "act", "act")
        AW_N = 512
        aw_slots = []
        for i in range(2):
            aw_slots.append((AR.f32([DC, AW_N]), abuf(f"aw{i}")))
        ab_slots = [(AR.f32([AW_N]), abuf(f"ab{i}")) for i in range(2)]
        gi = 0
        for l in layers:
            pt, pb = psum()
            for g in range(6 * D // AW_N):
                awt, awb = aw_slots[gi % 2]
                ab_row, b_ab = ab_slots[gi % 2]
                gi += 1
                S.dma("act", lambda e, l=l, g=g, ab_row=ab_row: e.dma_start(
                    out=ab_row[0:1, :], in_=ada_b[l:l + 1, g * AW_N:(g + 1) * AW_N]), [], [b_ab])
                S.dma("act", lambda e, l=l, g=g, awt=awt: e.dma_start(
                    out=awt, in_=ada_w[l][:, g * AW_N:(g + 1) * AW_N].rearrange("(k p) n -> p k n", p=128)), [], [awb])
                for jj in range(AW_N // 128):
                    j = g * (AW_N // 128) + jj
                    col = (j % 48) * 4
                    for k in range(DC):
                        S.op("pe", lambda e, pt=pt, awt=awt, jj=jj, k=k, col=col: e.matmul(
                            pt[:, col:col + 4], lhsT=awt[:, k, jj * 128:(jj + 1) * 128], rhs=sT[:, k:32:8],
                            start=(k == 0), stop=False), [awb, b_sT], [pb])
                    S.op("pe", lambda e, pt=pt, j=j, col=col: e.matmul(
                        pt[:, col:col + 4], lhsT=ab_row[0:1, jj * 128:(jj + 1) * 128], rhs=onesf[0:1, 0:4],
                        start=False, stop=True), [b_ab, B_const], [pb])
            S.op("act", lambda e, pt=pt, l=l: e.copy(out=MOD[:, l].rearrange("p j v -> p (j v)"), in_=pt[:, 0:192]),
                 [], [pb, B_mod])
        for l in layers:
            for n, which in ((0, 1), (1, 4)):
                for v in range(3):
                    S.op("dve", lambda e, l=l, n=n, which=which, v=v: e.scalar_tensor_tensor(
                        out=A12[:, n, l, :, v], in0=MOD[:, l, which * 8:(which + 1) * 8, v], scalar=1.0,
                        in1=NG[:, n, l, :], op0=ALU.add, op1=ALU.mult), [B_mod], [B_mod])

        def modcol(l, which, c, v):
            return MOD[:, l, which * 8 + c, v:v + 1]

        def vcol(seq, si):
            return 3 if False else (2 if si == 4 else seq)

        def load_x(seq):
            new_phase()
            slots = [(AR.f32([D]), abuf(f"xs{i}")) for i in range(3)]
            for tt in range(T // 128):
                stg, bs = slots[tt % 3]
                if tt < SL // 128:
                    src = x[seq, tt * 128:(tt + 1) * 128, :]
                else:
                    src = ctx[seq, (tt - 16) * 128:(tt - 15) * 128, :]
                S.dma("sp", lambda e, stg=stg, src=src: e.dma_start(out=stg, in_=src), [], [bs])
                si = min(tt // 4, 4)
                for half in range(2):
                    pt, pb = psum()
                    for cc in range(4):
                        c = half * 4 + cc
                        S.op("pe", lambda e, pt=pt, cc=cc, c=c, stg=stg: e.transpose(
                            out=pt[:, cc * 128:(cc + 1) * 128], in_=stg[:, c * 128:(c + 1) * 128], identity=ident[:]),
                            [bs, B_const], [pb])
                    eng = "act" if half == 0 else "dve"
                    wr = [B_XT[half * 4 + cc][si] for cc in range(4)]
                    dst = XT[:, half * 4:half * 4 + 4, tt * 128:(tt + 1) * 128]
                    srcp = pt[:, :].rearrange("p (c t) -> p c t", c=4)
                    if eng == "act":
                        S.op("act", lambda e, dst=dst, srcp=srcp: e.copy(out=dst, in_=srcp), [], [pb] + wr)
                    else:
                        S.op("dve", lambda e, dst=dst, srcp=srcp: e.tensor_copy(out=dst, in_=srcp), [], [pb] + wr)

        def store_x(seq):
            new_phase()
            slots = [(AR.f32([D]), abuf(f"os{i}")) for i in range(3)]
            for tt in range(SL // 128):
                stg, bs = slots[tt % 3]
                si = tt // 4
                for half in range(2):
                    pt, pb = psum()
                    for cc in range(4):
                        c = half * 4 + cc
                        S.op("pe", lambda e, pt=pt, cc=cc, c=c, tt=tt: e.transpose(
                            out=pt[:, cc * 128:(cc + 1) * 128], in_=XT[:, c, tt * 128:(tt + 1) * 128], identity=ident[:]),
                            [B_XT[c][si], B_const], [pb])
                    eng = "act" if half == 0 else "dve"
                    dst = stg[:, half * 512:(half + 1) * 512]
                    if eng == "act":
                        S.op("act", lambda e, dst=dst, pt=pt: e.copy(out=dst, in_=pt[:, :]), [], [pb, bs])
                    else:
                        S.op("dve", lambda e, dst=dst, pt=pt: e.tensor_copy(out=dst, in_=pt[:, :]), [], [pb, bs])
                S.dma("sp", lambda e, stg=stg, seq=seq, tt=tt: e.dma_start(out=out[seq, tt * 128:(tt + 1) * 128, :], in_=stg),
                      [bs], [])

        def norm_sub(l, n, seq, si, HTv, b_ht, SQ, b_sq, RS, b_rs, TMPs, hoff):
            t0, nt = SUBS[si]
            v = 2 if si == 4 else seq
            S.op("act", lambda e: e.activation(out=SQ[:, :, 0:nt], in_=XT[:, :, t0:t0 + nt], func=AF.Square),
                 [B_XT[c][si] for c in range(DC)], [b_sq])
            pt, pb = psum()
            for c in range(DC):
                S.op("pe", lambda e, pt=pt, c=c: e.matmul(pt[:, 0:nt], lhsT=onesd[:], rhs=SQ[:, c, 0:nt],
                                                           start=(c == 0), stop=(c == DC - 1)), [b_sq, B_const], [pb])
            S.op("act", lambda e, pt=pt: e.activation(out=RS[:, 0:nt], in_=pt[:, 0:nt], func=AF.Ln, bias=EPSC[:, 0:1], scale=1.0),
                 [B_const], [pb, b_rs])
            S.op("act", lambda e: e.activation(out=RS[:, 0:nt], in_=RS[:, 0:nt], func=AF.Exp, scale=-0.5), [], [b_rs])
            shw = 0 if n == 0 else 3
            for c in range(DC):
                tmp, btmp = TMPs[c % len(TMPs)]
                if c % 2 == 0:
                    S.op("dve", lambda e: e.tensor_tensor(out=tmp[:, 0:nt], in0=XT[:, c, t0:t0 + nt], in1=RS[:, 0:nt],
                                                          op=ALU.mult), [B_XT[c][si], b_rs], [btmp])
                    S.op("dve", lambda e: e.tensor_scalar(
                        out=HTv[:, c, hoff:hoff + nt], in0=tmp[:, 0:nt], scalar1=A12[:, n, l, c, v:v + 1],
                        scalar2=modcol(l, shw, c, v), op0=ALU.mult, op1=ALU.add), [btmp, B_mod], [b_ht])
                else:
                    S.op("pool", lambda e: e.tensor_tensor(out=tmp[:, 0:nt], in0=XT[:, c, t0:t0 + nt], in1=RS[:, 0:nt],
                                                           op=ALU.mult), [B_XT[c][si], b_rs], [btmp])
                    S.op("act", lambda e: e.activation(
                        out=HTv[:, c, hoff:hoff + nt], in_=tmp[:, 0:nt], func=AF.Identity,
                        scale=A12[:, n, l, c, v:v + 1], bias=modcol(l, shw, c, v)), [btmp, B_mod], [b_ht])

        def ffn_layer(l, seq, last=False):
            new_phase()
            subs = [0, 1, 2, 3] if last else [0, 1, 2, 3, 4]
            HS = [(AR.bf([DC, 512]), abuf(f"hts{i}")) for i in range(2)]
            GT = AR.bf([FC, 512]); b_gt = [abuf(f"gt{f}") for f in range(FC)]
            WGU = [(AR.bf([DC, GW_FFN]), AR.bf([DC, GW_FFN]), abuf(f"wg{i}"), abuf(f"wu{i}")) for i in range(3)]
            WDs = [(AR.bf([FC, 128]), abuf(f"wd{i}")) for i in range(3)]
            SQ = AR.bf([DC, 512]); b_sq = abuf("sq")
            SG = [(AR.bf([512]), abuf(f"sg{i}")) for i in range(2)]
            RS = AR.f32([512]); b_rs = abuf("rs")
            TMPs = [(AR.f32([512]), abuf(f"tmp{i}")) for i in range(4)]
            gcount = 0
            dcount = 0
            sgc = 0

            def do_norm(idx):
                si = subs[idx]
                HTs, b_hs = HS[idx % 2]
                norm_sub(l, 1, seq, si, HTs, b_hs, SQ, b_sq, RS, b_rs, TMPs, 0)

            do_norm(0)
            for idx, si in enumerate(subs):
                HTs, b_hs = HS[idx % 2]
                t0, nt = SUBS[si]
                v = 2 if si == 4 else seq
                for g in range(NG_FFN):
                    wg, wu, bwg, bwu = WGU[gcount % 3]
                    gcount += 1
                    S.dma("sp", lambda e: e.dma_start(out=wg, in_=scr_g[l][:, g]), [], [bwg])
                    S.dma("sp", lambda e: e.dma_start(out=wu, in_=scr_u[l][:, g]), [], [bwu])
                    for ff in range(GW_FFN // 128):
                        f = g * (GW_FFN // 128) + ff
                        pg, pgb = psum()
                        pu, pub = psum()
                        for k in range(DC):
                            S.op("pe", lambda e: e.matmul(
                                pg[:, 0:nt], lhsT=wg[:, k, ff * 128:(ff + 1) * 128], rhs=HTs[:, k, 0:nt],
                                start=(k == 0), stop=(k == DC - 1)), [bwg, b_hs], [pgb])
                        for k in range(DC):
                            S.op("pe", lambda e: e.matmul(
                                pu[:, 0:nt], lhsT=wu[:, k, ff * 128:(ff + 1) * 128], rhs=HTs[:, k, 0:nt],
                                start=(k == 0), stop=(k == DC - 1)), [bwu, b_hs], [pub])
                        sg, bsg = SG[sgc % 2]
                        sgc += 1
                        S.op("act", lambda e: e.activation(out=sg[:, 0:nt], in_=pg[:, 0:nt], func=AF.Silu), [], [pgb, bsg])
                        S.op("dve", lambda e: e.tensor_tensor(out=GT[:, f, 0:nt], in0=pu[:, 0:nt], in1=sg[:, 0:nt], op=ALU.mult),
                             [bsg], [pub, b_gt[f]])
                if idx + 1 < len(subs):
                    do_norm(idx + 1)
                for dc in range(DC):
                    wd, bwd = WDs[dcount % 3]
                    dcount += 1
                    S.dma("sp", lambda e: e.dma_start(out=wd, in_=scr_d[l][:, dc]), [], [bwd])
                    pd_, pdb_ = psum()
                    for f in range(FC):
                        S.op("pe", lambda e: e.matmul(
                            pd_[:, 0:nt], lhsT=wd[:, f, :], rhs=GT[:, f, 0:nt], start=(f == 0), stop=(f == FC - 1)),
                            [bwd, b_gt[f]], [pdb_])
                    S.op("dve", lambda e: e.scalar_tensor_tensor(
                        out=XT[:, dc, t0:t0 + nt], in0=pd_[:, 0:nt], scalar=modcol(l, 5, dc, v), in1=XT[:, dc, t0:t0 + nt],
                        op0=ALU.mult, op1=ALU.add), [B_mod], [pdb_, B_XT[dc][si]])

        B_small = Buf("small")

        def norm_full(l, seq):
            HT = AR.bf([DC, T])
            b_ht = [abuf(f"ht{si}") for si in range(5)]
            mark = AR.off
            SQ = AR.bf([DC, 512]); b_sq = abuf("sq")
            RS = AR.f32([512]); b_rs = abuf("rs")
            TMPs = [(AR.f32([512]), abuf(f"tmp{i}")) for i in range(2)]
            for si in range(5):
                norm_sub(l, 0, seq, si, HT, b_ht[si], SQ, b_sq, RS, b_rs, TMPs, SUBS[si][0])
            scratch_bufs = [b_sq, b_rs] + [t[1] for t in TMPs]
            AR.off = mark
            phase_old.extend(scratch_bufs)
            return HT, b_ht, scratch_bufs

        def abuf2(name, olds):
            b = abuf(name)
            inherit([b], olds)
            return b

        def pool_layer(l, seq, last):
            new_phase()
            HT, b_ht, olds = norm_full(l, seq)
            PW = AR.bf([4, 2, 256]); b_pw = abuf2("pw", olds)
            S.dma("pool", lambda e: e.dma_start(out=PW, in_=pool_w[0].rearrange("g (kc p) n -> p g kc n", p=128)), [], [b_pw])
            load_vec_fm([pool_scale[0].rearrange("(c p) -> c p", p=128)], SMALL[:, 0:8], 8, B_small)
            for v in range(3):
                S.op("dve", lambda e, v=v: e.tensor_tensor(out=SMALL[:, 8 + v * 8:16 + v * 8], in0=SMALL[:, 0:8],
                                                           in1=MOD[:, l, 16:24, v], op=ALU.mult), [B_mod, B_small], [B_small])
            PD = AR.bf([DC, T]); b_pd = [abuf2(f"pd{c}", olds) for c in range(DC)]
            ZS = {en: [(AR.f32([SL + 16]), abuf2(f"z{en}{i}", olds)) for i in range(2)] for en in ("dve", "pool")}
            for c in (6, 7, 4, 5, 2, 3, 0, 1):
                w = (2, 4, 8, 16)[c // 2]
                hw_ = w // 2
                peng = "pool" if c in (0, 2, 4, 6) else "dve"
                for (off, L) in ((0, SL), (SL, SCX)):
                    sis = [0, 1, 2, 3] if off == 0 else [4]
                    (za, bza), (zb, bzb) = ZS[peng]
                    S.op("pool", lambda e, za=za: e.memset(za[:, 0:8], 0.0), [], [bza])
                    S.op("pool", lambda e, za=za, L=L: e.memset(za[:, 8 + L:16 + L], 0.0), [], [bza])
                    S.op("act", lambda e, za=za, c=c, off=off, L=L: e.copy(out=za[:, 8:8 + L], in_=HT[:, c, off:off + L]),
                         [b_ht[si] for si in sis], [bza])
                    cur, bcur, oth, both = za, bza, zb, bzb
                    m = 1
                    while m < w:
                        n = L + 16 - m
                        S.op(peng, lambda e, cur=cur, oth=oth, m=m, n=n: e.tensor_tensor(
                            out=oth[:, 0:n], in0=cur[:, 0:n], in1=cur[:, m:m + n], op=ALU.add), [bcur], [both])
                        cur, bcur, oth, both = oth, both, cur, bcur
                        m *= 2
                    S.op("dve", lambda e, cur=cur, c=c, off=off, L=L, hw_=hw_, w=w: e.scalar_tensor_tensor(
                        out=PD[:, c, off:off + L], in0=cur[:, 8 - hw_:8 - hw_ + L], scalar=1.0 / w, in1=HT[:, c, off:off + L],
                        op0=ALU.mult, op1=ALU.subtract), [bcur] + [b_ht[si] for si in sis], [b_pd[c]])
                    edge = [(t, t + hw_) for t in range(hw_)] + [(t, L - t + hw_) for t in range(L - hw_ + 1, L)]
                    for (t, cnt) in edge:
                        S.op("dve", lambda e, cur=cur, c=c, off=off, t=t, cnt=cnt, hw_=hw_: e.scalar_tensor_tensor(
                            out=PD[:, c, off + t:off + t + 1], in0=cur[:, 8 - hw_ + t:9 - hw_ + t], scalar=1.0 / cnt,
                            in1=HT[:, c, off + t:off + t + 1], op0=ALU.mult, op1=ALU.subtract), [bcur], [b_pd[c]])
            for gi_ in range(4):
                for m in range(2):
                    oc = 2 * gi_ + m
                    for si in range(5):
                        t0, nt = SUBS[si]
                        v = 2 if si == 4 else seq
                        pt, pb = psum()
                        for kc in range(2):
                            S.op("pe", lambda e, pt=pt, gi_=gi_, kc=kc, m=m, t0=t0, nt=nt: e.matmul(
                                pt[:, 0:nt], lhsT=PW[:, gi_, kc, m * 128:(m + 1) * 128], rhs=PD[:, 2 * gi_ + kc, t0:t0 + nt],
                                start=(kc == 0), stop=(kc == 1)), [b_pw, b_pd[2 * gi_ + kc]], [pb])
                        S.op("dve", lambda e, pt=pt, oc=oc, t0=t0, nt=nt, v=v: e.scalar_tensor_tensor(
                            out=XT[:, oc, t0:t0 + nt], in0=pt[:, 0:nt], scalar=SMALL[:, 8 + v * 8 + oc:9 + v * 8 + oc],
                            in1=XT[:, oc, t0:t0 + nt], op0=ALU.mult, op1=ALU.add), [B_small], [pb, B_XT[oc][si]])

        def gmlp_layer(l, seq, last):
            new_phase()
            HT, b_ht, olds = norm_full(l, seq)
            WIN = AR.bf([4, DC, 512]); b_win = abuf2("win", olds)
            WOUT = AR.bf([2, DC, 512]); b_wout = abuf2("wout", olds)
            for g in range(4):
                S.dma("sp", lambda e, g=g: e.dma_start(out=WIN[:, g], in_=scr_gin[:, g]), [], [b_win])
            for g in range(2):
                S.dma("sp", lambda e, g=g: e.dma_start(out=WOUT[:, g], in_=scr_gout[:, g]), [], [b_wout])
            WSS = AR.f32([8, 128]); b_wss = abuf2("wss", olds)
            WST = AR.bf([8, 128]); b_wst = abuf2("wst", olds)
            S.dma("sp", lambda e: e.dma_start(out=WSS, in_=gm_w_s[0].rearrange("g p q -> p g q")), [], [b_wss])
            for hf in range(2):
                pt, pb = psum()
                for gg in range(4):
                    g = hf * 4 + gg
                    S.op("pe", lambda e, pt=pt, gg=gg, g=g: e.transpose(out=pt[:, gg * 128:(gg + 1) * 128], in_=WSS[:, g, :],
                                                                         identity=ident[:]), [b_wss, B_const], [pb])
                S.op("act", lambda e, pt=pt, hf=hf: e.copy(
                    out=WST[:, hf * 4:hf * 4 + 4, :], in_=pt[:, :].rearrange("p (g q) -> p g q", g=4)), [], [pb, b_wst])
            VG = AR.f32([D]); b_vg = abuf2("vg", olds)
            BSB = VG
            S.dma("sp", lambda e: e.dma_start(out=BSB, in_=gm_b_s[0].rearrange("g p -> (g p)").partition_broadcast(128)), [], [b_vg])
            load_vec_fm([gm_ln_b[0].rearrange("(c p) -> c p", p=128)], SMALL[:, 50:58], 8, B_small)
            CT = AR.f32([8, 128]); b_ct = abuf2("ct", olds)
            for hf in range(2):
                pt, pb = psum()
                S.op("pe", lambda e, pt=pt, hf=hf: e.matmul(
                    pt[:, :], lhsT=onesb[:, :], rhs=WST[:, hf * 4:hf * 4 + 4, :].rearrange("p g q -> p (g q)"),
                    start=True, stop=True), [b_wst, B_const], [pb])
                for gg in range(4):
                    g = hf * 4 + gg
                    S.op("dve", lambda e, pt=pt, gg=gg, g=g: e.scalar_tensor_tensor(
                        out=CT[:, g, :], in0=pt[:, gg * 128:(gg + 1) * 128], scalar=SMALL[:, 50 + g:51 + g],
                        in1=BSB[:, g * 128:(g + 1) * 128], op0=ALU.mult, op1=ALU.add), [B_small, b_vg], [pb, b_ct])
            LNG = AR.f32([D]); b_lng = abuf2("lng", olds)
            S.dma("sp", lambda e: e.dma_start(out=LNG, in_=gm_ln_g[0].partition_broadcast(128)), [], [b_lng])
            load_vec_fm([gm_b_in[0, 0:D].rearrange("(c p) -> c p", p=128)], SMALL[:, 32:40], 8, B_small)
            BVR = BVRT; b_bvr = B_bvr
            S.dma("pool", lambda e: e.dma_start(out=BVR[0:1, :], in_=gm_b_in[0:1, D:2 * D]), [], [b_bvr])
            UT = AR.bf([DC, 512]); b_ut = abuf2("ut", olds)
            GM = AR.bf([DC, 512]); b_gm = abuf2("gm", olds)
            VGs = [(VG, b_vg), (WSS.rearrange("p g q -> p (g q)"), b_wss)]
            VHs = [(AR.bf([D]), abuf2(f"vh{i}", olds)) for i in range(2)]
            TM = AR.f32([512]); b_tm = abuf2("tm", olds)
            STs = [SMALL[:, 40:48], SMALL[:, 64:72]]
            B_sts = [B_small3, B_small2]

            def u_proj(si):
                t0, nt = SUBS[si]
                for fc in range(DC):
                    pt, pb = psum()
                    for k in range(DC):
                        S.op("pe", lambda e: e.matmul(
                            pt[:, 0:nt], lhsT=WIN[:, fc // 4, k, (fc % 4) * 128:(fc % 4 + 1) * 128], rhs=HT[:, k, t0:t0 + nt],
                            start=(k == 0), stop=(k == DC - 1)), [b_win, b_ht[si]], [pb])
                    S.op("act", lambda e: e.activation(
                        out=UT[:, fc, 0:nt], in_=pt[:, 0:nt], func=AF.Gelu_apprx_tanh, bias=SMALL[:, 32 + fc:33 + fc], scale=1.0),
                        [B_small], [pb, b_ut])

            def stage1(ch):
                si, tc, slot = ch
                t0, nt = SUBS[si]
                tok0 = t0 + tc * 128
                VGc, b_vgc = VGs[slot]
                VHc, b_vhc = VHs[slot]
                ST = STs[slot]; bst = B_sts[slot]
                for hf in range(2):
                    pt, pb = psum()
                    for k in range(DC):
                        S.op("pe", lambda e: e.matmul(
                            pt[:, :], lhsT=HT[:, k, tok0:tok0 + 128], rhs=WIN[:, 2 + hf, k, :],
                            start=(k == 0), stop=False), [b_win, b_ht[si]], [pb])
                    S.op("pe", lambda e: e.matmul(
                        pt[:, :], lhsT=onesb[0:1, :], rhs=BVR[0:1, hf * 512:(hf + 1) * 512], start=False, stop=True),
                        [b_bvr, B_const], [pb])
                    S.op("act", lambda e: e.activation(
                        out=VGc[:, hf * 512:(hf + 1) * 512], in_=pt[:, :], func=AF.Gelu_apprx_tanh), [], [pb, b_vgc])
                S.op("act", lambda e: e.activation(out=VHc, in_=VGc, func=AF.Square), [b_vgc], [b_vhc])
                S.op("dve", lambda e: e.reduce_sum(out=ST[:, 0:1], in_=VGc, axis=mybir.AxisListType.X), [b_vgc], [bst])
                S.op("dve", lambda e: e.reduce_sum(out=ST[:, 2:3], in_=VHc, axis=mybir.AxisListType.X), [b_vhc], [bst])
                S.op("dve", lambda e: e.tensor_scalar(out=ST[:, 3:4], in0=ST[:, 0:1], scalar1=1.0 / D, scalar2=None,
                                                      op0=ALU.mult), [], [bst])
                S.op("dve", lambda e: e.tensor_tensor(out=ST[:, 4:5], in0=ST[:, 3:4], in1=ST[:, 3:4], op=ALU.mult), [], [bst])
                S.op("dve", lambda e: e.scalar_tensor_tensor(out=ST[:, 5:6], in0=ST[:, 2:3], scalar=1.0 / D, in1=ST[:, 4:5],
                                                             op0=ALU.mult, op1=ALU.subtract), [], [bst])
                S.op("act", lambda e: e.activation(out=ST[:, 6:7], in_=ST[:, 5:6], func=AF.Sqrt, bias=EPS, scale=1.0),
                     [], [bst])
                S.op("dve", lambda e: e.reciprocal(out=ST[:, 6:7], in_=ST[:, 6:7]), [], [bst])
                S.op("dve", lambda e: e.tensor_scalar(out=VGc, in0=VGc, scalar1=ST[:, 3:4], scalar2=ST[:, 6:7],
                                                      op0=ALU.subtract, op1=ALU.mult), [bst], [b_vgc])
                S.op("pool", lambda e: e.tensor_tensor(out=VHc, in0=VGc, in1=LNG, op=ALU.mult), [b_vgc, b_lng], [b_vhc])

            def stage2(ch):
                si, tc, slot = ch
                VHc, b_vhc = VHs[slot]
                for hf in range(2):
                    pt, pb = psum()
                    for gg in range(4):
                        g = hf * 4 + gg
                        S.op("pe", lambda e: e.matmul(
                            pt[:, gg * 128:(gg + 1) * 128], lhsT=VHc[:, g * 128:(g + 1) * 128], rhs=WST[:, g, :],
                            start=True, stop=True), [b_vhc, b_wst], [pb])
                    S.op("dve", lambda e: e.tensor_tensor(
                        out=TM, in0=pt[:, :], in1=CT[:, hf * 4:hf * 4 + 4, :].rearrange("p g q -> p (g q)"), op=ALU.add),
                        [b_ct], [pb, b_tm])
                    S.op("pool", lambda e: e.tensor_tensor(
                        out=GM[:, hf * 4:hf * 4 + 4, tc * 128:(tc + 1) * 128],
                        in0=TM.rearrange("p (g q) -> p g q", g=4),
                        in1=UT[:, hf * 4:hf * 4 + 4, tc * 128:(tc + 1) * 128], op=ALU.mult), [b_tm, b_ut], [b_gm])

            def out_proj(si):
                t0, nt = SUBS[si]
                v = 2 if si == 4 else seq
                for dc in range(DC):
                    pt, pb = psum()
                    for k in range(DC):
                        S.op("pe", lambda e: e.matmul(
                            pt[:, 0:nt], lhsT=WOUT[:, dc // 4, k, (dc % 4) * 128:(dc % 4 + 1) * 128], rhs=GM[:, k, 0:nt],
                            start=(k == 0), stop=(k == DC - 1)), [b_wout, b_gm], [pb])
                    S.op("dve", lambda e: e.scalar_tensor_tensor(
                        out=XT[:, dc, t0:t0 + nt], in0=pt[:, 0:nt], scalar=modcol(l, 2, dc, v), in1=XT[:, dc, t0:t0 + nt],
                        op0=ALU.mult, op1=ALU.add), [B_mod], [pb, B_XT[dc][si]])

            chunks = []
            for si in range(5):
                for tc in range(SUBS[si][1] // 128):
                    chunks.append((si, tc, len(chunks) % 2))
            u_proj(0)
            stage1(chunks[0])
            for i, ch in enumerate(chunks):
                if i + 1 < len(chunks):
                    stage1(chunks[i + 1])
                stage2(ch)
                si = ch[0]
                if i + 1 == len(chunks) or chunks[i + 1][0] != si:
                    out_proj(si)
                    if si + 1 < 5:
                        u_proj(si + 1)

        def na_jobs(qb):
            jobs = []
            for kt in range(16):
                rows = []
                for r in range(8 * qb, 8 * qb + 8):
                    rs = min(max(r - 4, 0), 24)
                    if 2 * kt + 1 >= rs and 2 * kt <= rs + 7:
                        interior = 4 <= r <= 28
                        idx = (3 - 2 * kt + r) if interior else (15 - 2 * kt + r)
                        rows.append((r, interior, idx))
                runs = []
                for (r, it, idx) in rows:
                    if runs and runs[-1][1] == r - 1 and runs[-1][3] == it:
                        runs[-1][1] = r
                    else:
                        runs.append([r, r, idx, it])
                if runs:
                    jobs.append((kt, [(a, b, i0) for (a, b, i0, _) in runs]))
            return jobs

        def na_layer(l, seq, last):
            j = l // 3
            new_phase()
            HT, b_ht, olds = norm_full(l, seq)
            nsub = 4 if last else 5
            for col, src_, mul in ((48, na_q_norm, HD ** -0.5), (49, na_k_norm, 1.0)):
                for hb in range(2):
                    S.dma("sp", lambda e, col=col, src_=src_, hb=hb: e.dma_start(
                        out=SMALL[hb * 64:(hb + 1) * 64, col:col + 1], in_=src_[j].rearrange("(p o) -> p o", o=1)), [], [B_small])
            S.op("dve", lambda e: e.tensor_scalar(out=SMALL[:, 48:49], in0=SMALL[:, 48:49], scalar1=HD ** -0.5, scalar2=None,
                                                  op0=ALU.mult), [], [B_small])
            WS = AR.bf([DC, 512]); b_ws = abuf2("ws", olds)
            QH = AR.bf([4, T]); b_qh = [[abuf2(f"qh{c}_{s_}", olds) for s_ in range(5)] for c in range(4)]
            KH = AR.bf([4, T]); b_kh = [abuf2(f"kh{c}", olds) for c in range(4)]
            VHf = AR.bf([18, 512]); b_vh = [abuf2(f"vh{t}", olds) for t in range(18)]
            NEG = AR.bf([NB_TAB * 64]); b_neg = abuf2("neg", olds)
            BT = [(AR.bf([NB_TAB * 64]), abuf2(f"bt{i}", olds)) for i in range(2)]
            ES = [(AR.bf([512]), abuf2(f"e{i}", olds)) for i in range(9)]
            SQ1 = AR.bf([512]); b_sq1 = abuf2("sq1", olds)
            RS1 = AR.f32([512]); b_rs1 = abuf2("rs1", olds)
            RD = AR.f32([512]); b_rd = abuf2("rd", olds)
            S.dma("pool", lambda e: e.dma_start(out=NEG, in_=na_mask), [], [b_neg])
            QZ = []
            for i in range(2):
                pair = []
                for ph in range(2):
                    qt = AR.bf([512]); qtb = abuf2(f"qz{i}{ph}", olds)
                    S.op("pool", lambda e, qt=qt: e.memset(qt, 0.0), [], [qtb])
                    pair.append((qt, qtb))
                QZ.append(pair)
            qzc = [0]
            ecnt = [0]
            scnt = [0]

            def headnorm(pt, pb, nt, gcol, dst, dbufs):
                S.op("act", lambda e: e.activation(out=SQ1[:, 0:nt], in_=pt[:, 0:nt], func=AF.Square), [], [pb, b_sq1])
                p2, p2b = psum()
                S.op("pe", lambda e: e.matmul(p2[:, 0:nt], lhsT=ones64[:], rhs=SQ1[:, 0:nt], start=True, stop=True),
                     [b_sq1, B_const], [p2b])
                S.op("act", lambda e: e.activation(out=RS1[:, 0:nt], in_=p2[:, 0:nt], func=AF.Ln, bias=EPSC[:, 0:1], scale=1.0),
                     [B_const], [p2b, b_rs1])
                S.op("act", lambda e: e.activation(out=RS1[:, 0:nt], in_=RS1[:, 0:nt], func=AF.Exp, scale=-0.5), [], [b_rs1])
                S.op("dve", lambda e: e.scalar_tensor_tensor(out=dst, in0=pt[:, 0:nt], scalar=SMALL[:, gcol:gcol + 1],
                                                             in1=RS1[:, 0:nt], op0=ALU.mult, op1=ALU.mult),
                     [b_rs1, B_small], [pb] + dbufs)

            for hh in range(2):
                for (grp, kind) in ((hh, "q"), (2 + hh, "k"), (4 + hh, "v")):
                    S.dma("sp", lambda e, grp=grp: e.dma_start(out=WS, in_=scr_qkv[j][:, grp]), [], [b_ws])
                    if kind in ("q", "k"):
                        pitems = [(cc, si) for cc in range(4) for si in range(nsub if kind == "q" else 5)]
                        pend = []

                        def proj(cc, si):
                            t0, nt = SUBS[si]
                            pt, pb = psum()
                            for k in range(DC):
                                S.op("pe", lambda e: e.matmul(
                                    pt[:, 0:nt], lhsT=WS[:, k, cc * 128:(cc + 1) * 128], rhs=HT[:, k, t0:t0 + nt],
                                    start=(k == 0), stop=(k == DC - 1)), [b_ws, b_ht[si]], [pb])
                            return (cc, si, pt, pb)

                        def fin(item):
                            cc, si, pt, pb = item
                            t0, nt = SUBS[si]
                            if kind == "q":
                                headnorm(pt, pb, nt, 48, QH[:, cc, t0:t0 + nt], [b_qh[cc][si]])
                            else:
                                headnorm(pt, pb, nt, 49, KH[:, cc, t0:t0 + nt], [b_kh[cc]])

                        for idx_, (cc, si) in enumerate(pitems):
                            pend.append(proj(cc, si))
                            if len(pend) > 2:
                                fin(pend.pop(0))
                        while pend:
                            fin(pend.pop(0))
                    else:
                        for tt in range(18):
                            si = min(tt // 4, 4)
                            pt, pb = psum()
                            for k in range(DC):
                                S.op("pe", lambda e, pt=pt, k=k, tt=tt: e.matmul(
                                    pt[:, :], lhsT=HT[:, k, tt * 128:(tt + 1) * 128], rhs=WS[:, k, :],
                                    start=(k == 0), stop=(k == DC - 1)), [b_ws, b_ht[si]], [pb])
                            S.op("act", lambda e, pt=pt, tt=tt: e.copy(out=VHf[:, tt, :], in_=pt[:, :]), [], [pb, b_vh[tt]])
                ps_pool[0] = [0, 1, 2, 3]
                po = [PSB[4], PSB[5]]; pob = [PSBUF[4], PSBUF[5]]
                pd = [PSB[6], PSB[7]]; pdb = [PSBUF[6], PSBUF[7]]
                items = []
                for cc in range(4):
                    for qb in range(nsub):
                        q0, N = SUBS[qb]
                        blkd = {"cc": cc, "qb": qb, "q0": q0, "N": N, "newpair": qb == 0}
                        tiles = [(16, [(None, None, None)]), (17, [(None, None, None)])]
                        if qb < 4:
                            tiles += na_jobs(qb)
                        js = []
                        for (kt, runs) in tiles:
                            for (ra, rb, idx0) in runs:
                                if ra is None:
                                    c0, n = 0, N
                                else:
                                    c0, n = (ra - 8 * qb) * 64, 64 * (rb - ra + 1)
                                for ph in range(2):
                                    js.append({"blk": blkd, "kt": kt, "c0": c0, "n": n, "idx0": idx0, "ph": ph,
                                               "firstb": False, "lastb": False, "start": kt == 16})
                        js[0]["firstb"] = True
                        js[-1]["lastb"] = True
                        items += js

                def stage_a(it):
                    blkd = it["blk"]; cc = blkd["cc"]; qb = blkd["qb"]; q0 = blkd["q0"]; N = blkd["N"]
                    if it["firstb"]:
                        if blkd["newpair"]:
                            for ph in range(2):
                                h = hh * 8 + cc * 2 + ph
                                bt, bbt = BT[ph]
                                S.dma("pool", lambda e: e.dma_start(out=bt, in_=na_bias[j, h]), [], [bbt])
                                S.op("dve", lambda e: e.tensor_tensor(out=bt, in0=bt, in1=NEG, op=ALU.add), [b_neg], [bbt])
                                S.op("act", lambda e: e.activation(out=bt, in_=bt, func=AF.Exp), [], [bbt])
                        qz = QZ[qzc[0] % 2]
                        qzc[0] += 1
                        blkd["qz"] = qz
                        for ph in range(2):
                            pl = slice(ph * 64, (ph + 1) * 64)
                            qt, qtb = qz[ph]
                            S.op("pool", lambda e: e.tensor_copy(out=qt[pl, 0:N], in_=QH[pl, cc, q0:q0 + N]),
                                 [b_qh[cc][qb]], [qtb])
                    kt = it["kt"]; c0 = it["c0"]; n = it["n"]; idx0 = it["idx0"]; ph = it["ph"]
                    ktok = 2048 + (kt - 16) * 128 if kt >= 16 else kt * 128
                    qt, qtb = blkd["qz"][ph]
                    sp_, spb = psum()
                    S.op("pe", lambda e: e.matmul(sp_[:, 0:n], lhsT=KH[:, cc, ktok:ktok + 128], rhs=qt[:, c0:c0 + n],
                                                  start=True, stop=True), [b_kh[cc], qtb], [spb])
                    et, ebt = ES[ecnt[0] % len(ES)]
                    ecnt[0] += 1
                    it["et"] = (et, ebt)
                    S.op("act", lambda e: e.activation(out=et[:, 0:n], in_=sp_[:, 0:n], func=AF.Exp), [], [spb, ebt])
                    if idx0 is not None:
                        bt, bbt = BT[ph]
                        meng = "dve"
                        scnt[0] += 1
                        S.op(meng, lambda e: e.tensor_tensor(out=et[:, 0:n], in0=et[:, 0:n], in1=bt[:, idx0 * 64:idx0 * 64 + n],
                                                             op=ALU.mult), [bbt], [ebt])

                def stage_b(it):
                    blkd = it["blk"]; cc = blkd["cc"]; qb = blkd["qb"]; q0 = blkd["q0"]; N = blkd["N"]
                    kt = it["kt"]; c0 = it["c0"]; n = it["n"]; ph = it["ph"]
                    et, ebt = it["et"]
                    f = it["start"]
                    pp = po[ph]; pq = pd[ph]
                    S.op("pe", lambda e: e.matmul(pp[:, c0:c0 + n], lhsT=VHf[:, kt, cc * 128:(cc + 1) * 128], rhs=et[:, 0:n],
                                                  start=f, stop=False, skip_group_check=True), [ebt, b_vh[kt]], [pob[ph]])
                    S.op("pe", lambda e: e.matmul(pq[:, c0:c0 + n], lhsT=onesb[:, :], rhs=et[:, 0:n],
                                                  start=f, stop=False, skip_group_check=True), [ebt, B_const], [pdb[ph]])
                    if it["lastb"]:
                        for ph2 in range(2):
                            pl = slice(ph2 * 64, (ph2 + 1) * 64)
                            pq2 = pd[ph2]
                            S.op("act", lambda e: e.activation(out=RD[pl, 0:N], in_=pq2[pl, 0:N], func=AF.Ln), [], [pdb[ph2], b_rd])
                        S.op("act", lambda e: e.activation(out=RD[:, 0:N], in_=RD[:, 0:N], func=AF.Exp, scale=-1.0), [], [b_rd])
                        for ph2 in range(2):
                            pl = slice(ph2 * 64, (ph2 + 1) * 64)
                            pp2 = po[ph2]
                            S.op("dve", lambda e: e.tensor_tensor(out=QH[pl, cc, q0:q0 + N], in0=pp2[pl, 0:N], in1=RD[pl, 0:N],
                                                                  op=ALU.mult), [b_rd], [pob[ph2], b_qh[cc][qb]])

                LOOK = 6
                for i in range(len(items) + LOOK):
                    if i < len(items):
                        stage_a(items[i])
                    if i >= LOOK:
                        stage_b(items[i - LOOK])
                ps_pool[0] = list(range(8))
                WO = WS.rearrange("p k n -> p (k n)").rearrange("p (g k n) -> p g k n", g=2, k=4)
                S.dma("sp", lambda e, hh=hh: e.dma_start(out=WO, in_=scr_wo[j][:, :, 4 * hh:4 * hh + 4, :]), [], [b_ws])
                for dc in range(DC):
                    for si in range(nsub):
                        t0, nt = SUBS[si]
                        v = 2 if si == 4 else seq
                        pt, pb = psum()
                        for cc in range(4):
                            S.op("pe", lambda e, pt=pt, dc=dc, cc=cc, t0=t0, nt=nt: e.matmul(
                                pt[:, 0:nt], lhsT=WO[:, dc // 4, cc, (dc % 4) * 128:(dc % 4 + 1) * 128], rhs=QH[:, cc, t0:t0 + nt],
                                start=(cc == 0), stop=(cc == 3)), [b_ws, b_qh[cc][si]], [pb])
                        S.op("dve", lambda e, pt=pt, dc=dc, t0=t0, nt=nt, v=v: e.scalar_tensor_tensor(
                            out=XT[:, dc, t0:t0 + nt], in0=pt[:, 0:nt], scalar=modcol(l, 2, dc, v), in1=XT[:, dc, t0:t0 + nt],
                            op0=ALU.mult, op1=ALU.add), [B_mod], [pb, B_XT[dc][si]])

        MIX = {'pool': pool_layer, 'gmlp': gmlp_layer, 'na': na_layer}

        for seq in range(nseq):
            load_x(seq)
            for l in layers:
                kind = l % 3
                last = (l == DEPTH - 1)
                if kind == 0 and "na" in cfg["mixers"]:
                    MIX["na"](l, seq, last)
                elif kind == 1 and "gmlp" in cfg["mixers"]:
                    MIX["gmlp"](l, seq, last)
                elif kind == 2 and "pool" in cfg["mixers"]:
                    MIX["pool"](l, seq, last)
                if cfg["ffn"]:
                    ffn_layer(l, seq, last)
            store_x(seq)
        S.dma_fence("sp")
        stats = S.emit()
    return nc, stats


NB_TAB = 23
MIXER_BUILDERS = {}


def _prep_inputs(inputs, core):
    b0 = core * NSEQ
    m = {}
    m["x"] = np.ascontiguousarray(inputs["x"][b0:b0 + NSEQ])
    m["c"] = np.ascontiguousarray(inputs["c"][b0:b0 + NSEQ])
    m["ctx"] = np.ascontiguousarray(inputs["ctx"][b0:b0 + NSEQ])
    for k in ("c_ctx", "ada_w", "ada_b", "norm1_g", "norm2_g", "ffn_w_gate", "ffn_w_up", "ffn_w_down",
              "na_w_qkv", "na_w_o", "na_q_norm", "na_k_norm", "gm_w_in", "gm_b_in", "gm_ln_g", "gm_ln_b",
              "gm_w_s", "gm_b_s", "gm_w_out", "pool_w", "pool_scale"):
        m[k] = np.ascontiguousarray(inputs[k], dtype=np.float32)
    return m


def kernel(**inputs):
    inputs = {k: np.asarray(v) for k, v in inputs.items()}
    nc, _ = build_nc(CFG)
    extra = na_tables(inputs["na_rpb"])
    in_maps = []
    for core in range(8):
        m = _prep_inputs(inputs, core)
        m.update(extra)
        in_maps.append(m)
    res = run_bass_kernel_spmd(nc, in_maps, core_ids=list(range(8)))
    return np.concatenate([r["out"] for r in res.results], axis=0).astype(np.float32)


def na_tables(rpb):
    rpb = np.asarray(rpb, dtype=np.float32)
    tab = np.zeros((2, H, 128, NB_TAB, 64), np.float32)
    neg = np.full((128, NB_TAB, 64), -30000.0, np.float32)
    qcol = np.arange(64)
    cstart = np.clip(qcol - 8, 0, 48)
    kcol = np.arange(64)
    ok_col = (kcol[:, None] >= cstart[None, :]) & (kcol[:, None] < cstart[None, :] + 16)
    dcm = np.clip(kcol[:, None] - qcol[None, :] + 15, 0, 30)
    for idx in range(NB_TAB):
        if idx < 9:
            delta, lo, hi = 3 - idx, -4, 3
        else:
            delta, lo, hi = 6 - (idx - 9), -7, 7
        for krl in range(2):
            dr = delta + krl
            if not (lo <= dr <= hi):
                continue
            g = rpb[:, :, dr + 7, :][:, :, dcm]
            sel = np.broadcast_to(ok_col, g.shape)
            blk = tab[:, :, krl * 64:(krl + 1) * 64, idx, :]
            blk[sel] = g[sel]
            neg[krl * 64:(krl + 1) * 64, idx, :][ok_col] = 0.0
    return {"na_bias": np.ascontiguousarray(tab.reshape(2, H, 128, NB_TAB * 64)),
            "na_mask": np.ascontiguousarray(neg.reshape(128, NB_TAB * 64))}
```

```python
import contextlib
import types
import numpy as np
import concourse.bass as bass
import concourse.mybir as mybir
from concourse.bass_utils import run_bass_kernel_spmd

F32 = mybir.dt.float32
BF16 = mybir.dt.bfloat16
AF = mybir.ActivationFunctionType
ALU = mybir.AluOpType

EPOCH = 12000
N_DMA_SEMS = 30
DMA_POOLS = {"sp": (0, 18), "act": (18, 8), "pool": (26, 4)}


class Buf:
    __slots__ = ("w", "r", "name")

    def __init__(self, name=""):
        self.w = None
        self.r = {}
        self.name = name


def _tok_key_val(tok):
    if tok[0] == "op":
        return tok[1].eng, tok[1].idx
    return ("dma", tok[1]), tok[2]


def inherit(new_bufs, old_bufs):
    merged = {}
    for b in old_bufs:
        toks = list(b.r.values())
        if b.w is not None:
            toks.append(b.w)
        for t in toks:
            k, v = _tok_key_val(t)
            if k not in merged or _tok_key_val(merged[k])[1] < v:
                merged[k] = t
    for nb in new_bufs:
        for k, t in merged.items():
            kk = ("inh", k)
            nb.r[kk] = t


class Op:
    __slots__ = ("eng", "fn", "waits", "idx", "need_inc", "dma_sem", "dma_val")


def _snapshot(fn):
    if fn.__closure__ is None:
        return fn
    cells = []
    for c in fn.__closure__:
        try:
            cells.append(types.CellType(c.cell_contents))
        except ValueError:
            cells.append(c)
    return types.FunctionType(fn.__code__, fn.__globals__, fn.__name__, fn.__defaults__, tuple(cells))


class Sched:
    ENGS = ("pe", "act", "dve", "pool", "sp")

    def __init__(self, nc):
        self.nc = nc
        self.q = {e: [] for e in self.ENGS}
        self.seen = {e: {} for e in self.ENGS}
        self.dma_vals = [0] * N_DMA_SEMS
        self.q_rr = {k: 0 for k in DMA_POOLS}

    def _add_wait(self, op, tok, waits):
        if tok is None:
            return
        if tok[0] == "op":
            src = tok[1]
            if src.eng == "pe" and op.eng == "pe":
                return
        key, val = _tok_key_val(tok)
        if self.seen[op.eng].get(key, -1) >= val:
            return
        cur = waits.get(key)
        if cur is None or cur[0] < val:
            waits[key] = (val, tok)

    def _mk(self, eng, fn, reads, writes, dma):
        op = Op()
        op.eng = eng
        op.fn = _snapshot(fn)
        op.need_inc = False
        op.idx = len(self.q[eng])
        op.dma_sem = None
        waits = {}
        for b in reads:
            self._add_wait(op, b.w, waits)
        for b in writes:
            self._add_wait(op, b.w, waits)
            for t in b.r.values():
                self._add_wait(op, t, waits)
        if dma:
            lo_, n_ = DMA_POOLS[eng]
            s = lo_ + self.q_rr[eng]
            self.q_rr[eng] = (self.q_rr[eng] + 1) % n_
            prev = self.dma_vals[s]
            if prev > 0:
                self._add_wait(op, ("dma", s, prev), waits)
            self.dma_vals[s] = prev + 16
            op.dma_sem = s
            op.dma_val = prev + 16
            tok = ("dma", s, prev + 16)
        else:
            tok = ("op", op)
        op.waits = []
        seen = self.seen[eng]
        for key, (val, t) in waits.items():
            seen[key] = val
            op.waits.append(t)
            if t[0] == "op":
                t[1].need_inc = True
        rkey = eng if not dma else ("dma", op.dma_sem)
        for b in reads:
            b.r[rkey] = tok
        for b in writes:
            b.w = tok
            b.r = {}
        self.q[eng].append(op)
        return op

    def op(self, eng, fn, reads=(), writes=()):
        return self._mk(eng, fn, reads, writes, False)

    def dma(self, eng, fn, reads=(), writes=()):
        return self._mk(eng, fn, reads, writes, True)

    def final_wait(self, eng, bufs):
        self._mk(eng, lambda e: e.nop(), list(bufs), list(bufs), False)

    def dma_fence(self, eng):
        fb = Buf("fence")
        for s in range(N_DMA_SEMS):
            if self.dma_vals[s] > 0:
                fb.r[("dma", s)] = ("dma", s, self.dma_vals[s])
        self._mk(eng, lambda e: e.nop(), [], [fb], False)

    def emit(self):
        nc = self.nc
        counts = {}
        n_epochs = {}
        for e in self.ENGS:
            c = 0
            for op in self.q[e]:
                if op.need_inc:
                    c += 1
                    counts[id(op)] = c
            n_epochs[e] = max(1, -(-c // EPOCH))
        with contextlib.ExitStack() as st:
            esems = {e: [st.enter_context(nc.semaphore(f"s_{e}_{i}")) for i in range(n_epochs[e])]
                     for e in self.ENGS}
            dsems = [st.enter_context(nc.semaphore(f"s_dma_{i}")) for i in range(N_DMA_SEMS)]
            block = st.enter_context(nc.Block())

            def resolve(tok):
                if tok[0] == "op":
                    c = counts[id(tok[1])]
                    ep = (c - 1) // EPOCH
                    return esems[tok[1].eng][ep], c - ep * EPOCH
                return dsems[tok[1]], tok[2]

            def run(ename):
                def body(e):
                    for op in self.q[ename]:
                        for t in op.waits:
                            s, v = resolve(t)
                            e.wait_ge(s, v)
                        ins = op.fn(e)
                        if op.dma_sem is not None:
                            ins.then_inc(dsems[op.dma_sem], 16)
                        elif op.need_inc:
                            c = counts[id(op)]
                            ep = (c - 1) // EPOCH
                            ins.then_inc(esems[ename][ep], 1)
                return body

            block.tensor(run("pe"))
            block.scalar(run("act"))
            block.vector(run("dve"))
            block.gpsimd(run("pool"))
            block.sync(run("sp"))
        return {e: len(self.q[e]) for e in self.ENGS}


D = 1024
DC = 8
FH = 2816
FC = 22
SL = 2048
SCX = 256
T = SL + SCX
DEPTH = 4
NSEQ = 2
H = 16
HD = 64
EPS = 1e-6
SUBS = [(0, 512), (512, 512), (1024, 512), (1536, 512), (2048, 256)]
SUPERS = [[0, 1], [2, 3, 4]]
GW_FFN = 256
NG_FFN = FH // GW_FFN

CFG = {"layers": [0, 1, 2, 3], "mixers": {"na", "gmlp", "pool"}, "ffn": True, "nseq": NSEQ}


class Arena:
    def __init__(self, tensor, nelem):
        self.t = tensor
        self.n = nelem
        self.off = 0
        self.marks = []

    def bf(self, shape):
        n = int(np.prod(shape))
        n = (n + 1) // 2 * 2
        assert self.off + n <= self.n, ("arena overflow", self.off, n, self.n)
        v = self.t[:, self.off:self.off + int(np.prod(shape))]
        self.off += n
        return _shape(v, shape)

    def f32(self, shape):
        n = int(np.prod(shape)) * 2
        assert self.off + n <= self.n, ("arena overflow", self.off, n, self.n)
        v = self.t[:, self.off:self.off + n].bitcast(F32)
        self.off += n
        return _shape(v, shape)

    def push(self):
        self.marks.append(self.off)

    def pop(self):
        self.off = self.marks.pop()


def _shape(v, shape):
    if len(shape) == 1:
        return v
    if len(shape) == 2:
        return v.rearrange("p (a b) -> p a b", a=shape[0])
    if len(shape) == 3:
        return v.rearrange("p (a b c) -> p a b c", a=shape[0], b=shape[1])
    raise ValueError(shape)


def build_nc(cfg=CFG):
    nc = bass.Bass("TRN2", target_bir_lowering=False)
    nseq = cfg["nseq"]

    def din(name, shape):
        return nc.dram_tensor(name, list(shape), F32, kind="ExternalInput").ap()

    x = din("x", [NSEQ, SL, D])
    c_in = din("c", [NSEQ, D])
    ctx = din("ctx", [NSEQ, SCX, D])
    c_ctx = din("c_ctx", [D])
    ada_w = din("ada_w", [DEPTH, D, 6 * D])
    ada_b = din("ada_b", [DEPTH, 6 * D])
    norm1_g = din("norm1_g", [DEPTH, D])
    norm2_g = din("norm2_g", [DEPTH, D])
    ffn_w_gate = din("ffn_w_gate", [DEPTH, D, FH])
    ffn_w_up = din("ffn_w_up", [DEPTH, D, FH])
    ffn_w_down = din("ffn_w_down", [DEPTH, FH, D])
    na_w_qkv = din("na_w_qkv", [2, D, 3 * D])
    na_w_o = din("na_w_o", [2, D, D])
    na_q_norm = din("na_q_norm", [2, HD])
    na_k_norm = din("na_k_norm", [2, HD])
    na_bias = din("na_bias", [2, H, 128, NB_TAB * 64])
    na_mask = din("na_mask", [128, NB_TAB * 64])
    gm_w_in = din("gm_w_in", [1, D, 2 * D])
    gm_b_in = din("gm_b_in", [1, 2 * D])
    gm_ln_g = din("gm_ln_g", [1, D])
    gm_ln_b = din("gm_ln_b", [1, D])
    gm_w_s = din("gm_w_s", [1, 8, 128, 128])
    gm_b_s = din("gm_b_s", [1, 8, 128])
    gm_w_out = din("gm_w_out", [1, D, D])
    pool_w = din("pool_w", [1, 4, 256, 256])
    pool_scale = din("pool_scale", [1, D])
    out = nc.dram_tensor("out", [NSEQ, SL, D], F32, kind="ExternalOutput").ap()

    def scratch(name, shape):
        return nc.dram_tensor(name, list(shape), BF16, kind="Internal").ap()

    layers = cfg["layers"]
    scr_g = {l: scratch(f"scr_g{l}", [128, NG_FFN, DC, GW_FFN]) for l in layers}
    scr_u = {l: scratch(f"scr_u{l}", [128, NG_FFN, DC, GW_FFN]) for l in layers}
    scr_d = {l: scratch(f"scr_d{l}", [128, DC, FC, 128]) for l in layers}
    scr_qkv = {j: scratch(f"scr_qkv{j}", [128, 6, DC, 512]) for j in range(2)}
    scr_wo = {j: scratch(f"scr_wo{j}", [128, 2, DC, 512]) for j in range(2)}
    scr_gin = scratch("scr_gin", [128, 4, DC, 512])
    scr_gout = scratch("scr_gout", [128, 2, DC, 512])

    with contextlib.ExitStack() as st:
        S = Sched(nc)
        AR_N = 64000
        arena_t = st.enter_context(nc.sbuf_tensor("arena", [128, AR_N], BF16))
        AR = Arena(arena_t, AR_N)
        XT = st.enter_context(nc.sbuf_tensor("XT", [128, DC, T], F32))
        ident = st.enter_context(nc.sbuf_tensor("ident", [128, 128], F32))
        onesd = st.enter_context(nc.sbuf_tensor("onesd", [128, 128], BF16))
        ones64 = st.enter_context(nc.sbuf_tensor("ones64", [128, 128], BF16))
        onesb = st.enter_context(nc.sbuf_tensor("onesb", [128, 128], BF16))
        onesf = st.enter_context(nc.sbuf_tensor("onesf", [128, 128], F32))
        MOD = st.enter_context(nc.sbuf_tensor("MOD", [128, DEPTH, 48, 4], F32))
        NG = st.enter_context(nc.sbuf_tensor("NG", [128, 2, DEPTH, DC], F32))
        A12 = st.enter_context(nc.sbuf_tensor("A12", [128, 2, DEPTH, DC, 4], F32))
        SMALL = st.enter_context(nc.sbuf_tensor("SMALL", [128, 80], F32))
        BVRT = st.enter_context(nc.sbuf_tensor("BVRT", [1, D], BF16))
        B_bvr = Buf("bvr")
        B_small2 = Buf("small2")
        B_small3 = Buf("small3")
        EPSC = st.enter_context(nc.sbuf_tensor("EPSC", [128, 2], F32))
        PSB = [st.enter_context(nc.psum_tensor(f"ps{i}", [128, 512], F32)) for i in range(8)]
        PSBUF = [Buf(f"ps{i}") for i in range(8)]
        ps_rr = [0]
        ps_pool = [list(range(8))]

        def psum():
            pool_ = ps_pool[0]
            i = pool_[ps_rr[0] % len(pool_)]
            ps_rr[0] += 1
            return PSB[i], PSBUF[i]

        B_XT = [[Buf(f"xt{c}_{s}") for s in range(len(SUBS))] for c in range(DC)]
        B_const = Buf("const")
        B_mod = Buf("mod")
        B_out = Buf("out")
        B_arena_all = []
        phase_old = []

        def new_phase():
            nonlocal B_arena_all, phase_old
            phase_old = B_arena_all
            B_arena_all = []
            AR.off = 0

        def abuf(name):
            b = Buf(name)
            inherit([b], phase_old)
            B_arena_all.append(b)
            return b

        S.op("pool", lambda e: e.memset(ident[:], 0.0), [], [B_const])
        S.op("pool", lambda e: e.affine_select(out=ident[:], in_=ident[:], pattern=[[-1, 128]],
                                               compare_op=ALU.not_equal, fill=1.0, base=0,
                                               channel_multiplier=1), [], [B_const])
        S.op("pool", lambda e: e.memset(EPSC[:], EPS), [], [B_const])
        S.op("pool", lambda e: e.memset(onesd[:], 1.0 / D), [], [B_const])
        S.op("pool", lambda e: e.memset(onesb[:], 1.0), [], [B_const])
        S.op("pool", lambda e: e.memset(onesf[:], 1.0), [], [B_const])
        S.op("pool", lambda e: e.memset(ones64[:], 0.0), [], [B_const])
        S.op("pool", lambda e: e.memset(ones64[0:64, 0:64], 1.0 / HD), [], [B_const])
        S.op("pool", lambda e: e.memset(ones64[64:128, 64:128], 1.0 / HD), [], [B_const])

        cast_rr = [0]

        pc_tiles = []

        def precast(src2d, K, N, dst, gw):
            for k in range(K):
                pc_tiles.append((src2d, k, N, dst, gw))

        def precast_emit(slots, depth=3):
            def load(i):
                src2d, k, N, dst, gw = pc_tiles[i]
                st32, stb, b32, bb = slots[i % len(slots)]
                S.dma("sp", lambda e: e.dma_start(out=st32[:, 0:N], in_=src2d[k * 128:(k + 1) * 128, :]), [], [b32])

            def cast_store(i):
                src2d, k, N, dst, gw = pc_tiles[i]
                st32, stb, b32, bb = slots[i % len(slots)]
                eng = ("dve", "dve", "dve", "pool")[i % 4]
                S.op(eng, lambda e: e.tensor_copy(out=stb[:, 0:N], in_=st32[:, 0:N]), [b32], [bb])
                S.dma("sp", lambda e: e.dma_start(out=dst[:, :, k, :], in_=stb[:, 0:N].rearrange("p (g c) -> p g c", c=gw)),
                      [bb], [])

            n = len(pc_tiles)
            for i in range(n + depth):
                if i < n:
                    load(i)
                if i >= depth:
                    cast_store(i - depth)

        new_phase()
        B_scr = Buf("scr")
        pc_slots = []
        for i in range(4):
            st32 = AR.f32([3072])
            stb = AR.bf([3072])
            pc_slots.append((st32, stb, abuf(f"pc32_{i}"), abuf(f"pcb_{i}")))
        pc_end = AR.off
        if cfg["ffn"]:
            for l in layers:
                precast(ffn_w_gate[l], DC, FH, scr_g[l], GW_FFN)
                precast(ffn_w_up[l], DC, FH, scr_u[l], GW_FFN)
                precast(ffn_w_down[l], FC, D, scr_d[l], 128)
        if "na" in cfg["mixers"]:
            for j in range(2):
                if (j * 3) in layers:
                    precast(na_w_qkv[j], DC, 3 * D, scr_qkv[j], 512)
                    precast(na_w_o[j], DC, D, scr_wo[j], 512)
        if "gmlp" in cfg["mixers"] and 1 in layers:
            precast(gm_w_in[0], DC, 2 * D, scr_gin, 512)
            precast(gm_w_out[0], DC, D, scr_gout, 512)

        precast_emit(pc_slots)
        S.dma_fence("sp")
        phase_old = []
        AR.off = pc_end
        stage = AR.f32([128])
        b_stage = abuf("stage")
        sT = AR.f32([32])
        b_sT = abuf("sT")
        for v in range(4):
            if v < NSEQ:
                src = c_in[v].rearrange("(c p) -> c p", p=128)
            else:
                src = c_ctx.rearrange("(c p) -> c p", p=128)
            S.dma("act", lambda e, v=v, src=src: e.dma_start(out=stage[v * 8:(v + 1) * 8, :], in_=src), [], [b_stage])
        pt, pb = psum()
        S.op("pe", lambda e, pt=pt: e.transpose(out=pt[:, 0:32], in_=stage[0:32, :], identity=ident[0:32, 0:32]),
             [b_stage, B_const], [pb])
        S.op("act", lambda e, pt=pt: e.activation(out=sT[:, 0:32], in_=pt[:, 0:32], func=AF.Silu), [], [pb, b_sT])
        def load_vec_fm(srcs, dst_ap, nrows, dbuf=None, dq="sp", ev="dve"):
            dbuf = dbuf or B_mod
            stg = AR.f32([128])
            bs = abuf("vstg")
            r0 = 0
            for s_ap in srcs:
                r = s_ap.shape[0]
                S.dma(dq, lambda e, s_ap=s_ap, r0=r0, r=r, stg=stg: e.dma_start(out=stg[r0:r0 + r, :], in_=s_ap), [], [bs])
                r0 += r
            assert r0 == nrows
            pt, pb = psum()
            S.op("pe", lambda e, pt=pt, stg=stg: e.transpose(out=pt[:, 0:nrows], in_=stg[0:nrows, :], identity=ident[0:nrows, 0:nrows]),
                 [bs, B_const], [pb])
            if ev == "act":
                S.op("act", lambda e, pt=pt: e.copy(out=dst_ap, in_=pt[:, 0:nrows]), [], [pb, dbuf])
            else:
                S.op("dve", lambda e, pt=pt: e.tensor_copy(out=dst_ap, in_=pt[:, 0:nrows]), [], [pb, dbuf])

        load_vec_fm([norm1_g.rearrange("l (c p) -> (l c) p", p=128), norm2_g.rearrange("l (c p) -> (l c) p", p=128)],
                    NG[:].rearrange("p a l c -> p (a l c)"), 2 * DEPTH * DC, None, "act", "act")
        AW_N = 512
        aw_slots = []
        for i in range(2):
            aw_slots.append((AR.f32([DC, AW_N]), abuf(f"aw{i}")))
        ab_slots = [(AR.f32([AW_N]), abuf(f"ab{i}")) for i in range(2)]
        gi = 0
        for l in layers:
            pt, pb = psum()
            for g in range(6 * D // AW_N):
                awt, awb = aw_slots[gi % 2]
                ab_row, b_ab = ab_slots[gi % 2]
                gi += 1
                S.dma("act", lambda e, l=l, g=g, ab_row=ab_row: e.dma_start(
                    out=ab_row[0:1, :], in_=ada_b[l:l + 1, g * AW_N:(g + 1) * AW_N]), [], [b_ab])
                S.dma("act", lambda e, l=l, g=g, awt=awt: e.dma_start(
                    out=awt, in_=ada_w[l][:, g * AW_N:(g + 1) * AW_N].rearrange("(k p) n -> p k n", p=128)), [], [awb])
                for jj in range(AW_N // 128):
                    j = g * (AW_N // 128) + jj
                    col = (j % 48) * 4
                    for k in range(DC):
                        S.op("pe", lambda e, pt=pt, awt=awt, jj=jj, k=k, col=col: e.matmul(
                            pt[:, col:col + 4], lhsT=awt[:, k, jj * 128:(jj + 1) * 128], rhs=sT[:, k:32:8],
                            start=(k == 0), stop=False), [awb, b_sT], [pb])
                    S.op("pe", lambda e, pt=pt, j=j, col=col: e.matmul(
                        pt[:, col:col + 4], lhsT=ab_row[0:1, jj * 128:(jj + 1) * 128], rhs=onesf[0:1, 0:4],
                        start=False, stop=True), [b_ab, B_const], [pb])
            S.op("act", lambda e, pt=pt, l=l: e.copy(out=MOD[:, l].rearrange("p j v -> p (j v)"), in_=pt[:, 0:192]),
                 [], [pb, B_mod])
        for l in layers:
            for n, which in ((0, 1), (1, 4)):
                for v in range(3):
                    S.op("dve", lambda e, l=l, n=n, which=which, v=v: e.scalar_tensor_tensor(
                        out=A12[:, n, l, :, v], in0=MOD[:, l, which * 8:(which + 1) * 8, v], scalar=1.0,
                        in1=NG[:, n, l, :], op0=ALU.add, op1=ALU.mult), [B_mod], [B_mod])

        def modcol(l, which, c, v):
            return MOD[:, l, which * 8 + c, v:v + 1]

        def vcol(seq, si):
            return 3 if False else (2 if si == 4 else seq)

        def load_x(seq):
            new_phase()
            slots = [(AR.f32([D]), abuf(f"xs{i}")) for i in range(3)]
            for tt in range(T // 128):
                stg, bs = slots[tt % 3]
                if tt < SL // 128:
                    src = x[seq, tt * 128:(tt + 1) * 128, :]
                else:
                    src = ctx[seq, (tt - 16) * 128:(tt - 15) * 128, :]
                S.dma("sp", lambda e, stg=stg, src=src: e.dma_start(out=stg, in_=src), [], [bs])
                si = min(tt // 4, 4)
                for half in range(2):
                    pt, pb = psum()
                    for cc in range(4):
                        c = half * 4 + cc
                        S.op("pe", lambda e, pt=pt, cc=cc, c=c, stg=stg: e.transpose(
                            out=pt[:, cc * 128:(cc + 1) * 128], in_=stg[:, c * 128:(c + 1) * 128], identity=ident[:]),
                            [bs, B_const], [pb])
                    eng = "act" if half == 0 else "dve"
                    wr = [B_XT[half * 4 + cc][si] for cc in range(4)]
                    dst = XT[:, half * 4:half * 4 + 4, tt * 128:(tt + 1) * 128]
                    srcp = pt[:, :].rearrange("p (c t) -> p c t", c=4)
                    if eng == "act":
                        S.op("act", lambda e, dst=dst, srcp=srcp: e.copy(out=dst, in_=srcp), [], [pb] + wr)
                    else:
                        S.op("dve", lambda e, dst=dst, srcp=srcp: e.tensor_copy(out=dst, in_=srcp), [], [pb] + wr)

        def store_x(seq):
            new_phase()
            slots = [(AR.f32([D]), abuf(f"os{i}")) for i in range(3)]
            for tt in range(SL // 128):
                stg, bs = slots[tt % 3]
                si = tt // 4
                for half in range(2):
                    pt, pb = psum()
                    for cc in range(4):
                        c = half * 4 + cc
                        S.op("pe", lambda e, pt=pt, cc=cc, c=c, tt=tt: e.transpose(
                            out=pt[:, cc * 128:(cc + 1) * 128], in_=XT[:, c, tt * 128:(tt + 1) * 128], identity=ident[:]),
                            [B_XT[c][si], B_const], [pb])
                    eng = "act" if half == 0 else "dve"
                    dst = stg[:, half * 512:(half + 1) * 512]
                    if eng == "act":
                        S.op("act", lambda e, dst=dst, pt=pt: e.copy(out=dst, in_=pt[:, :]), [], [pb, bs])
                    else:
                        S.op("dve", lambda e, dst=dst, pt=pt: e.tensor_copy(out=dst, in_=pt[:, :]), [], [pb, bs])
                S.dma("sp", lambda e, stg=stg, seq=seq, tt=tt: e.dma_start(out=out[seq, tt * 128:(tt + 1) * 128, :], in_=stg),
                      [bs], [])

        def norm_sub(l, n, seq, si, HTv, b_ht, SQ, b_sq, RS, b_rs, TMPs, hoff):
            t0, nt = SUBS[si]
            v = 2 if si == 4 else seq
            S.op("act", lambda e: e.activation(out=SQ[:, :, 0:nt], in_=XT[:, :, t0:t0 + nt], func=AF.Square),
                 [B_XT[c][si] for c in range(DC)], [b_sq])
            pt, pb = psum()
            for c in range(DC):
                S.op("pe", lambda e, pt=pt, c=c: e.matmul(pt[:, 0:nt], lhsT=onesd[:], rhs=SQ[:, c, 0:nt],
                                                           start=(c == 0), stop=(c == DC - 1)), [b_sq, B_const], [pb])
            S.op("act", lambda e, pt=pt: e.activation(out=RS[:, 0:nt], in_=pt[:, 0:nt], func=AF.Ln, bias=EPSC[:, 0:1], scale=1.0),
                 [B_const], [pb, b_rs])
            S.op("act", lambda e: e.activation(out=RS[:, 0:nt], in_=RS[:, 0:nt], func=AF.Exp, scale=-0.5), [], [b_rs])
            shw = 0 if n == 0 else 3
            for c in range(DC):
                tmp, btmp = TMPs[c % len(TMPs)]
                if c % 2 == 0:
                    S.op("dve", lambda e: e.tensor_tensor(out=tmp[:, 0:nt], in0=XT[:, c, t0:t0 + nt], in1=RS[:, 0:nt],
                                                          op=ALU.mult), [B_XT[c][si], b_rs], [btmp])
                    S.op("dve", lambda e: e.tensor_scalar(
                        out=HTv[:, c, hoff:hoff + nt], in0=tmp[:, 0:nt], scalar1=A12[:, n, l, c, v:v + 1],
                        scalar2=modcol(l, shw, c, v), op0=ALU.mult, op1=ALU.add), [btmp, B_mod], [b_ht])
                else:
                    S.op("pool", lambda e: e.tensor_tensor(out=tmp[:, 0:nt], in0=XT[:, c, t0:t0 + nt], in1=RS[:, 0:nt],
                                                           op=ALU.mult), [B_XT[c][si], b_rs], [btmp])
                    S.op("act", lambda e: e.activation(
                        out=HTv[:, c, hoff:hoff + nt], in_=tmp[:, 0:nt], func=AF.Identity,
                        scale=A12[:, n, l, c, v:v + 1], bias=modcol(l, shw, c, v)), [btmp, B_mod], [b_ht])

        def ffn_layer(l, seq, last=False):
            new_phase()
            subs = [0, 1, 2, 3] if last else [0, 1, 2, 3, 4]
            HS = [(AR.bf([DC, 512]), abuf(f"hts{i}")) for i in range(2)]
            GT = AR.bf([FC, 512]); b_gt = [abuf(f"gt{f}") for f in range(FC)]
            WGU = [(AR.bf([DC, GW_FFN]), AR.bf([DC, GW_FFN]), abuf(f"wg{i}"), abuf(f"wu{i}")) for i in range(3)]
            WDs = [(AR.bf([FC, 128]), abuf(f"wd{i}")) for i in range(3)]
            SQ = AR.bf([DC, 512]); b_sq = abuf("sq")
            SG = [(AR.bf([512]), abuf(f"sg{i}")) for i in range(2)]
            RS = AR.f32([512]); b_rs = abuf("rs")
            TMPs = [(AR.f32([512]), abuf(f"tmp{i}")) for i in range(4)]
            gcount = 0
            dcount = 0
            sgc = 0

            def do_norm(idx):
                si = subs[idx]
                HTs, b_hs = HS[idx % 2]
                norm_sub(l, 1, seq, si, HTs, b_hs, SQ, b_sq, RS, b_rs, TMPs, 0)

            do_norm(0)
            for idx, si in enumerate(subs):
                HTs, b_hs = HS[idx % 2]
                t0, nt = SUBS[si]
                v = 2 if si == 4 else seq
                for g in range(NG_FFN):
                    wg, wu, bwg, bwu = WGU[gcount % 3]
                    gcount += 1
                    S.dma("sp", lambda e: e.dma_start(out=wg, in_=scr_g[l][:, g]), [], [bwg])
                    S.dma("sp", lambda e: e.dma_start(out=wu, in_=scr_u[l][:, g]), [], [bwu])
                    for ff in range(GW_FFN // 128):
                        f = g * (GW_FFN // 128) + ff
                        pg, pgb = psum()
                        pu, pub = psum()
                        for k in range(DC):
                            S.op("pe", lambda e: e.matmul(
                                pg[:, 0:nt], lhsT=wg[:, k, ff * 128:(ff + 1) * 128], rhs=HTs[:, k, 0:nt],
                                start=(k == 0), stop=(k == DC - 1)), [bwg, b_hs], [pgb])
                        for k in range(DC):
                            S.op("pe", lambda e: e.matmul(
                                pu[:, 0:nt], lhsT=wu[:, k, ff * 128:(ff + 1) * 128], rhs=HTs[:, k, 0:nt],
                                start=(k == 0), stop=(k == DC - 1)), [bwu, b_hs], [pub])
                        sg, bsg = SG[sgc % 2]
                        sgc += 1
                        S.op("act", lambda e: e.activation(out=sg[:, 0:nt], in_=pg[:, 0:nt], func=AF.Silu), [], [pgb, bsg])
                        S.op("dve", lambda e: e.tensor_tensor(out=GT[:, f, 0:nt], in0=pu[:, 0:nt], in1=sg[:, 0:nt], op=ALU.mult),
                             [bsg], [pub, b_gt[f]])
                if idx + 1 < len(subs):
                    do_norm(idx + 1)
                for dc in range(DC):
                    wd, bwd = WDs[dcount % 3]
                    dcount += 1
                    S.dma("sp", lambda e: e.dma_start(out=wd, in_=scr_d[l][:, dc]), [], [bwd])
                    pd_, pdb_ = psum()
                    for f in range(FC):
                        S.op("pe", lambda e: e.matmul(
                            pd_[:, 0:nt], lhsT=wd[:, f, :], rhs=GT[:, f, 0:nt], start=(f == 0), stop=(f == FC - 1)),
                            [bwd, b_gt[f]], [pdb_])
                    S.op("dve", lambda e: e.scalar_tensor_tensor(
                        out=XT[:, dc, t0:t0 + nt], in0=pd_[:, 0:nt], scalar=modcol(l, 5, dc, v), in1=XT[:, dc, t0:t0 + nt],
                        op0=ALU.mult, op1=ALU.add), [B_mod], [pdb_, B_XT[dc][si]])

        B_small = Buf("small")

        def norm_full(l, seq):
            HT = AR.bf([DC, T])
            b_ht = [abuf(f"ht{si}") for si in range(5)]
            mark = AR.off
            SQ = AR.bf([DC, 512]); b_sq = abuf("sq")
            RS = AR.f32([512]); b_rs = abuf("rs")
            TMPs = [(AR.f32([512]), abuf(f"tmp{i}")) for i in range(2)]
            for si in range(5):
                norm_sub(l, 0, seq, si, HT, b_ht[si], SQ, b_sq, RS, b_rs, TMPs, SUBS[si][0])
            scratch_bufs = [b_sq, b_rs] + [t[1] for t in TMPs]
            AR.off = mark
            phase_old.extend(scratch_bufs)
            return HT, b_ht, scratch_bufs

        def abuf2(name, olds):
            b = abuf(name)
            inherit([b], olds)
            return b

        def pool_layer(l, seq, last):
            new_phase()
            HT, b_ht, olds = norm_full(l, seq)
            PW = AR.bf([4, 2, 256]); b_pw = abuf2("pw", olds)
            S.dma("pool", lambda e: e.dma_start(out=PW, in_=pool_w[0].rearrange("g (kc p) n -> p g kc n", p=128)), [], [b_pw])
            load_vec_fm([pool_scale[0].rearrange("(c p) -> c p", p=128)], SMALL[:, 0:8], 8, B_small)
            for v in range(3):
                S.op("dve", lambda e, v=v: e.tensor_tensor(out=SMALL[:, 8 + v * 8:16 + v * 8], in0=SMALL[:, 0:8],
                                                           in1=MOD[:, l, 16:24, v], op=ALU.mult), [B_mod, B_small], [B_small])
            PD = AR.bf([DC, T]); b_pd = [abuf2(f"pd{c}", olds) for c in range(DC)]
            ZS = {en: [(AR.f32([SL + 16]), abuf2(f"z{en}{i}", olds)) for i in range(2)] for en in ("dve", "pool")}
            for c in (6, 7, 4, 5, 2, 3, 0, 1):
                w = (2, 4, 8, 16)[c // 2]
                hw_ = w // 2
                peng = "pool" if c in (0, 2, 4, 6) else "dve"
                for (off, L) in ((0, SL), (SL, SCX)):
                    sis = [0, 1, 2, 3] if off == 0 else [4]
                    (za, bza), (zb, bzb) = ZS[peng]
                    S.op("pool", lambda e, za=za: e.memset(za[:, 0:8], 0.0), [], [bza])
                    S.op("pool", lambda e, za=za, L=L: e.memset(za[:, 8 + L:16 + L], 0.0), [], [bza])
                    S.op("act", lambda e, za=za, c=c, off=off, L=L: e.copy(out=za[:, 8:8 + L], in_=HT[:, c, off:off + L]),
                         [b_ht[si] for si in sis], [bza])
                    cur, bcur, oth, both = za, bza, zb, bzb
                    m = 1
                    while m < w:
                        n = L + 16 - m
                        S.op(peng, lambda e, cur=cur, oth=oth, m=m, n=n: e.tensor_tensor(
                            out=oth[:, 0:n], in0=cur[:, 0:n], in1=cur[:, m:m + n], op=ALU.add), [bcur], [both])
                        cur, bcur, oth, both = oth, both, cur, bcur
                        m *= 2
                    S.op("dve", lambda e, cur=cur, c=c, off=off, L=L, hw_=hw_, w=w: e.scalar_tensor_tensor(
                        out=PD[:, c, off:off + L], in0=cur[:, 8 - hw_:8 - hw_ + L], scalar=1.0 / w, in1=HT[:, c, off:off + L],
                        op0=ALU.mult, op1=ALU.subtract), [bcur] + [b_ht[si] for si in sis], [b_pd[c]])
                    edge = [(t, t + hw_) for t in range(hw_)] + [(t, L - t + hw_) for t in range(L - hw_ + 1, L)]
                    for (t, cnt) in edge:
                        S.op("dve", lambda e, cur=cur, c=c, off=off, t=t, cnt=cnt, hw_=hw_: e.scalar_tensor_tensor(
                            out=PD[:, c, off + t:off + t + 1], in0=cur[:, 8 - hw_ + t:9 - hw_ + t], scalar=1.0 / cnt,
                            in1=HT[:, c, off + t:off + t + 1], op0=ALU.mult, op1=ALU.subtract), [bcur], [b_pd[c]])
            for gi_ in range(4):
                for m in range(2):
                    oc = 2 * gi_ + m
                    for si in range(5):
                        t0, nt = SUBS[si]
                        v = 2 if si == 4 else seq
                        pt, pb = psum()
                        for kc in range(2):
                            S.op("pe", lambda e, pt=pt, gi_=gi_, kc=kc, m=m, t0=t0, nt=nt: e.matmul(
                                pt[:, 0:nt], lhsT=PW[:, gi_, kc, m * 128:(m + 1) * 128], rhs=PD[:, 2 * gi_ + kc, t0:t0 + nt],
                                start=(kc == 0), stop=(kc == 1)), [b_pw, b_pd[2 * gi_ + kc]], [pb])
                        S.op("dve", lambda e, pt=pt, oc=oc, t0=t0, nt=nt, v=v: e.scalar_tensor_tensor(
                            out=XT[:, oc, t0:t0 + nt], in0=pt[:, 0:nt], scalar=SMALL[:, 8 + v * 8 + oc:9 + v * 8 + oc],
                            in1=XT[:, oc, t0:t0 + nt], op0=ALU.mult, op1=ALU.add), [B_small], [pb, B_XT[oc][si]])

        def gmlp_layer(l, seq, last):
            new_phase()
            HT, b_ht, olds = norm_full(l, seq)
            WIN = AR.bf([4, DC, 512]); b_win = abuf2("win", olds)
            WOUT = AR.bf([2, DC, 512]); b_wout = abuf2("wout", olds)
            for g in range(4):
                S.dma("sp", lambda e, g=g: e.dma_start(out=WIN[:, g], in_=scr_gin[:, g]), [], [b_win])
            for g in range(2):
                S.dma("sp", lambda e, g=g: e.dma_start(out=WOUT[:, g], in_=scr_gout[:, g]), [], [b_wout])
            WSS = AR.f32([8, 128]); b_wss = abuf2("wss", olds)
            WST = AR.bf([8, 128]); b_wst = abuf2("wst", olds)
            S.dma("sp", lambda e: e.dma_start(out=WSS, in_=gm_w_s[0].rearrange("g p q -> p g q")), [], [b_wss])
            for hf in range(2):
                pt, pb = psum()
                for gg in range(4):
                    g = hf * 4 + gg
                    S.op("pe", lambda e, pt=pt, gg=gg, g=g: e.transpose(out=pt[:, gg * 128:(gg + 1) * 128], in_=WSS[:, g, :],
                                                                         identity=ident[:]), [b_wss, B_const], [pb])
                S.op("act", lambda e, pt=pt, hf=hf: e.copy(
                    out=WST[:, hf * 4:hf * 4 + 4, :], in_=pt[:, :].rearrange("p (g q) -> p g q", g=4)), [], [pb, b_wst])
            VG = AR.f32([D]); b_vg = abuf2("vg", olds)
            BSB = VG
            S.dma("sp", lambda e: e.dma_start(out=BSB, in_=gm_b_s[0].rearrange("g p -> (g p)").partition_broadcast(128)), [], [b_vg])
            load_vec_fm([gm_ln_b[0].rearrange("(c p) -> c p", p=128)], SMALL[:, 50:58], 8, B_small)
            CT = AR.f32([8, 128]); b_ct = abuf2("ct", olds)
            for hf in range(2):
                pt, pb = psum()
                S.op("pe", lambda e, pt=pt, hf=hf: e.matmul(
                    pt[:, :], lhsT=onesb[:, :], rhs=WST[:, hf * 4:hf * 4 + 4, :].rearrange("p g q -> p (g q)"),
                    start=True, stop=True), [b_wst, B_const], [pb])
                for gg in range(4):
                    g = hf * 4 + gg
                    S.op("dve", lambda e, pt=pt, gg=gg, g=g: e.scalar_tensor_tensor(
                        out=CT[:, g, :], in0=pt[:, gg * 128:(gg + 1) * 128], scalar=SMALL[:, 50 + g:51 + g],
                        in1=BSB[:, g * 128:(g + 1) * 128], op0=ALU.mult, op1=ALU.add), [B_small, b_vg], [pb, b_ct])
            LNG = AR.f32([D]); b_lng = abuf2("lng", olds)
            S.dma("sp", lambda e: e.dma_start(out=LNG, in_=gm_ln_g[0].partition_broadcast(128)), [], [b_lng])
            load_vec_fm([gm_b_in[0, 0:D].rearrange("(c p) -> c p", p=128)], SMALL[:, 32:40], 8, B_small)
            BVR = BVRT; b_bvr = B_bvr
            S.dma("pool", lambda e: e.dma_start(out=BVR[0:1, :], in_=gm_b_in[0:1, D:2 * D]), [], [b_bvr])
            UT = AR.bf([DC, 512]); b_ut = abuf2("ut", olds)
            GM = AR.bf([DC, 512]); b_gm = abuf2("gm", olds)
            VGs = [(VG, b_vg), (WSS.rearrange("p g q -> p (g q)"), b_wss)]
            VHs = [(AR.bf([D]), abuf2(f"vh{i}", olds)) for i in range(2)]
            TM = AR.f32([512]); b_tm = abuf2("tm", olds)
            STs = [SMALL[:, 40:48], SMALL[:, 64:72]]
            B_sts = [B_small3, B_small2]

            def u_proj(si):
                t0, nt = SUBS[si]
                for fc in range(DC):
                    pt, pb = psum()
                    for k in range(DC):
                        S.op("pe", lambda e: e.matmul(
                            pt[:, 0:nt], lhsT=WIN[:, fc // 4, k, (fc % 4) * 128:(fc % 4 + 1) * 128], rhs=HT[:, k, t0:t0 + nt],
                            start=(k == 0), stop=(k == DC - 1)), [b_win, b_ht[si]], [pb])
                    S.op("act", lambda e: e.activation(
                        out=UT[:, fc, 0:nt], in_=pt[:, 0:nt], func=AF.Gelu_apprx_tanh, bias=SMALL[:, 32 + fc:33 + fc], scale=1.0),
                        [B_small], [pb, b_ut])

            def stage1(ch):
                si, tc, slot = ch
                t0, nt = SUBS[si]
                tok0 = t0 + tc * 128
                VGc, b_vgc = VGs[slot]
                VHc, b_vhc = VHs[slot]
                ST = STs[slot]; bst = B_sts[slot]
                for hf in range(2):
                    pt, pb = psum()
                    for k in range(DC):
                        S.op("pe", lambda e: e.matmul(
                            pt[:, :], lhsT=HT[:, k, tok0:tok0 + 128], rhs=WIN[:, 2 + hf, k, :],
                            start=(k == 0), stop=False), [b_win, b_ht[si]], [pb])
                    S.op("pe", lambda e: e.matmul(
                        pt[:, :], lhsT=onesb[0:1, :], rhs=BVR[0:1, hf * 512:(hf + 1) * 512], start=False, stop=True),
                        [b_bvr, B_const], [pb])
                    S.op("act", lambda e: e.activation(
                        out=VGc[:, hf * 512:(hf + 1) * 512], in_=pt[:, :], func=AF.Gelu_apprx_tanh), [], [pb, b_vgc])
                S.op("act", lambda e: e.activation(out=VHc, in_=VGc, func=AF.Square), [b_vgc], [b_vhc])
                S.op("dve", lambda e: e.reduce_sum(out=ST[:, 0:1], in_=VGc, axis=mybir.AxisListType.X), [b_vgc], [bst])
                S.op("dve", lambda e: e.reduce_sum(out=ST[:, 2:3], in_=VHc, axis=mybir.AxisListType.X), [b_vhc], [bst])
                S.op("dve", lambda e: e.tensor_scalar(out=ST[:, 3:4], in0=ST[:, 0:1], scalar1=1.0 / D, scalar2=None,
                                                      op0=ALU.mult), [], [bst])
                S.op("dve", lambda e: e.tensor_tensor(out=ST[:, 4:5], in0=ST[:, 3:4], in1=ST[:, 3:4], op=ALU.mult), [], [bst])
                S.op("dve", lambda e: e.scalar_tensor_tensor(out=ST[:, 5:6], in0=ST[:, 2:3], scalar=1.0 / D, in1=ST[:, 4:5],
                                                             op0=ALU.mult, op1=ALU.subtract), [], [bst])
                S.op("act", lambda e: e.activation(out=ST[:, 6:7], in_=ST[:, 5:6], func=AF.Sqrt, bias=EPS, scale=1.0),
                     [], [bst])
                S.op("dve", lambda e: e.reciprocal(out=ST[:, 6:7], in_=ST[:, 6:7]), [], [bst])
                S.op("dve", lambda e: e.tensor_scalar(out=VGc, in0=VGc, scalar1=ST[:, 3:4], scalar2=ST[:, 6:7],
                                                      op0=ALU.subtract, op1=ALU.mult), [bst], [b_vgc])
                S.op("pool", lambda e: e.tensor_tensor(out=VHc, in0=VGc, in1=LNG, op=ALU.mult), [b_vgc, b_lng], [b_vhc])

            def stage2(ch):
                si, tc, slot = ch
                VHc, b_vhc = VHs[slot]
                for hf in range(2):
                    pt, pb = psum()
                    for gg in range(4):
                        g = hf * 4 + gg
                        S.op("pe", lambda e: e.matmul(
                            pt[:, gg * 128:(gg + 1) * 128], lhsT=VHc[:, g * 128:(g + 1) * 128], rhs=WST[:, g, :],
                            start=True, stop=True), [b_vhc, b_wst], [pb])
                    S.op("dve", lambda e: e.tensor_tensor(
                        out=TM, in0=pt[:, :], in1=CT[:, hf * 4:hf * 4 + 4, :].rearrange("p g q -> p (g q)"), op=ALU.add),
                        [b_ct], [pb, b_tm])
                    S.op("pool", lambda e: e.tensor_tensor(
                        out=GM[:, hf * 4:hf * 4 + 4, tc * 128:(tc + 1) * 128],
                        in0=TM.rearrange("p (g q) -> p g q", g=4),
                        in1=UT[:, hf * 4:hf * 4 + 4, tc * 128:(tc + 1) * 128], op=ALU.mult), [b_tm, b_ut], [b_gm])

            def out_proj(si):
                t0, nt = SUBS[si]
                v = 2 if si == 4 else seq
                for dc in range(DC):
                    pt, pb = psum()
                    for k in range(DC):
                        S.op("pe", lambda e: e.matmul(
                            pt[:, 0:nt], lhsT=WOUT[:, dc // 4, k, (dc % 4) * 128:(dc % 4 + 1) * 128], rhs=GM[:, k, 0:nt],
                            start=(k == 0), stop=(k == DC - 1)), [b_wout, b_gm], [pb])
                    S.op("dve", lambda e: e.scalar_tensor_tensor(
                        out=XT[:, dc, t0:t0 + nt], in0=pt[:, 0:nt], scalar=modcol(l, 2, dc, v), in1=XT[:, dc, t0:t0 + nt],
                        op0=ALU.mult, op1=ALU.add), [B_mod], [pb, B_XT[dc][si]])

            chunks = []
            for si in range(5):
                for tc in range(SUBS[si][1] // 128):
                    chunks.append((si, tc, len(chunks) % 2))
            u_proj(0)
            stage1(chunks[0])
            for i, ch in enumerate(chunks):
                if i + 1 < len(chunks):
                    stage1(chunks[i + 1])
                stage2(ch)
                si = ch[0]
                if i + 1 == len(chunks) or chunks[i + 1][0] != si:
                    out_proj(si)
                    if si + 1 < 5:
                        u_proj(si + 1)

        def na_jobs(qb):
            jobs = []
            for kt in range(16):
                rows = []
                for r in range(8 * qb, 8 * qb + 8):
                    rs = min(max(r - 4, 0), 24)
                    if 2 * kt + 1 >= rs and 2 * kt <= rs + 7:
                        interior = 4 <= r <= 28
                        idx = (3 - 2 * kt + r) if interior else (15 - 2 * kt + r)
                        rows.append((r, interior, idx))
                runs = []
                for (r, it, idx) in rows:
                    if runs and runs[-1][1] == r - 1 and runs[-1][3] == it:
                        runs[-1][1] = r
                    else:
                        runs.append([r, r, idx, it])
                if runs:
                    jobs.append((kt, [(a, b, i0) for (a, b, i0, _) in runs]))
            return jobs

        def na_layer(l, seq, last):
            j = l // 3
            new_phase()
            HT, b_ht, olds = norm_full(l, seq)
            nsub = 4 if last else 5
            for col, src_, mul in ((48, na_q_norm, HD ** -0.5), (49, na_k_norm, 1.0)):
                for hb in range(2):
                    S.dma("sp", lambda e, col=col, src_=src_, hb=hb: e.dma_start(
                        out=SMALL[hb * 64:(hb + 1) * 64, col:col + 1], in_=src_[j].rearrange("(p o) -> p o", o=1)), [], [B_small])
            S.op("dve", lambda e: e.tensor_scalar(out=SMALL[:, 48:49], in0=SMALL[:, 48:49], scalar1=HD ** -0.5, scalar2=None,
                                                  op0=ALU.mult), [], [B_small])
            WS = AR.bf([DC, 512]); b_ws = abuf2("ws", olds)
            QH = AR.bf([4, T]); b_qh = [[abuf2(f"qh{c}_{s_}", olds) for s_ in range(5)] for c in range(4)]
            KH = AR.bf([4, T]); b_kh = [abuf2(f"kh{c}", olds) for c in range(4)]
            VHf = AR.bf([18, 512]); b_vh = [abuf2(f"vh{t}", olds) for t in range(18)]
            NEG = AR.bf([NB_TAB * 64]); b_neg = abuf2("neg", olds)
            BT = [(AR.bf([NB_TAB * 64]), abuf2(f"bt{i}", olds)) for i in range(2)]
            ES = [(AR.bf([512]), abuf2(f"e{i}", olds)) for i in range(9)]
            SQ1 = AR.bf([512]); b_sq1 = abuf2("sq1", olds)
            RS1 = AR.f32([512]); b_rs1 = abuf2("rs1", olds)
            RD = AR.f32([512]); b_rd = abuf2("rd", olds)
            S.dma("pool", lambda e: e.dma_start(out=NEG, in_=na_mask), [], [b_neg])
            QZ = []
            for i in range(2):
                pair = []
                for ph in range(2):
                    qt = AR.bf([512]); qtb = abuf2(f"qz{i}{ph}", olds)
                    S.op("pool", lambda e, qt=qt: e.memset(qt, 0.0), [], [qtb])
                    pair.append((qt, qtb))
                QZ.append(pair)
            qzc = [0]
            ecnt = [0]
            scnt = [0]

            def headnorm(pt, pb, nt, gcol, dst, dbufs):
                S.op("act", lambda e: e.activation(out=SQ1[:, 0:nt], in_=pt[:, 0:nt], func=AF.Square), [], [pb, b_sq1])
                p2, p2b = psum()
                S.op("pe", lambda e: e.matmul(p2[:, 0:nt], lhsT=ones64[:], rhs=SQ1[:, 0:nt], start=True, stop=True),
                     [b_sq1, B_const], [p2b])
                S.op("act", lambda e: e.activation(out=RS1[:, 0:nt], in_=p2[:, 0:nt], func=AF.Ln, bias=EPSC[:, 0:1], scale=1.0),
                     [B_const], [p2b, b_rs1])
                S.op("act", lambda e: e.activation(out=RS1[:, 0:nt], in_=RS1[:, 0:nt], func=AF.Exp, scale=-0.5), [], [b_rs1])
                S.op("dve", lambda e: e.scalar_tensor_tensor(out=dst, in0=pt[:, 0:nt], scalar=SMALL[:, gcol:gcol + 1],
                                                             in1=RS1[:, 0:nt], op0=ALU.mult, op1=ALU.mult),
                     [b_rs1, B_small], [pb] + dbufs)

            for hh in range(2):
                for (grp, kind) in ((hh, "q"), (2 + hh, "k"), (4 + hh, "v")):
                    S.dma("sp", lambda e, grp=grp: e.dma_start(out=WS, in_=scr_qkv[j][:, grp]), [], [b_ws])
                    if kind in ("q", "k"):
                        pitems = [(cc, si) for cc in range(4) for si in range(nsub if kind == "q" else 5)]
                        pend = []

                        def proj(cc, si):
                            t0, nt = SUBS[si]
                            pt, pb = psum()
                            for k in range(DC):
                                S.op("pe", lambda e: e.matmul(
                                    pt[:, 0:nt], lhsT=WS[:, k, cc * 128:(cc + 1) * 128], rhs=HT[:, k, t0:t0 + nt],
                                    start=(k == 0), stop=(k == DC - 1)), [b_ws, b_ht[si]], [pb])
                            return (cc, si, pt, pb)

                        def fin(item):
                            cc, si, pt, pb = item
                            t0, nt = SUBS[si]
                            if kind == "q":
                                headnorm(pt, pb, nt, 48, QH[:, cc, t0:t0 + nt], [b_qh[cc][si]])
                            else:
                                headnorm(pt, pb, nt, 49, KH[:, cc, t0:t0 + nt], [b_kh[cc]])

                        for idx_, (cc, si) in enumerate(pitems):
                            pend.append(proj(cc, si))
                            if len(pend) > 2:
                                fin(pend.pop(0))
                        while pend:
                            fin(pend.pop(0))
                    else:
                        for tt in range(18):
                            si = min(tt // 4, 4)
                            pt, pb = psum()
                            for k in range(DC):
                                S.op("pe", lambda e, pt=pt, k=k, tt=tt: e.matmul(
                                    pt[:, :], lhsT=HT[:, k, tt * 128:(tt + 1) * 128], rhs=WS[:, k, :],
                                    start=(k == 0), stop=(k == DC - 1)), [b_ws, b_ht[si]], [pb])
                            S.op("act", lambda e, pt=pt, tt=tt: e.copy(out=VHf[:, tt, :], in_=pt[:, :]), [], [pb, b_vh[tt]])
                ps_pool[0] = [0, 1, 2, 3]
                po = [PSB[4], PSB[5]]; pob = [PSBUF[4], PSBUF[5]]
                pd = [PSB[6], PSB[7]]; pdb = [PSBUF[6], PSBUF[7]]
                items = []
                for cc in range(4):
                    for qb in range(nsub):
                        q0, N = SUBS[qb]
                        blkd = {"cc": cc, "qb": qb, "q0": q0, "N": N, "newpair": qb == 0}
                        tiles = [(16, [(None, None, None)]), (17, [(None, None, None)])]
                        if qb < 4:
                            tiles += na_jobs(qb)
                        js = []
                        for (kt, runs) in tiles:
                            for (ra, rb, idx0) in runs:
                                if ra is None:
                                    c0, n = 0, N
                                else:
                                    c0, n = (ra - 8 * qb) * 64, 64 * (rb - ra + 1)
                                for ph in range(2):
                                    js.append({"blk": blkd, "kt": kt, "c0": c0, "n": n, "idx0": idx0, "ph": ph,
                                               "firstb": False, "lastb": False, "start": kt == 16})
                        js[0]["firstb"] = True
                        js[-1]["lastb"] = True
                        items += js

                def stage_a(it):
                    blkd = it["blk"]; cc = blkd["cc"]; qb = blkd["qb"]; q0 = blkd["q0"]; N = blkd["N"]
                    if it["firstb"]:
                        if blkd["newpair"]:
                            for ph in range(2):
                                h = hh * 8 + cc * 2 + ph
                                bt, bbt = BT[ph]
                                S.dma("pool", lambda e: e.dma_start(out=bt, in_=na_bias[j, h]), [], [bbt])
                                S.op("dve", lambda e: e.tensor_tensor(out=bt, in0=bt, in1=NEG, op=ALU.add), [b_neg], [bbt])
                                S.op("act", lambda e: e.activation(out=bt, in_=bt, func=AF.Exp), [], [bbt])
                        qz = QZ[qzc[0] % 2]
                        qzc[0] += 1
                        blkd["qz"] = qz
                        for ph in range(2):
                            pl = slice(ph * 64, (ph + 1) * 64)
                            qt, qtb = qz[ph]
                            S.op("pool", lambda e: e.tensor_copy(out=qt[pl, 0:N], in_=QH[pl, cc, q0:q0 + N]),
                                 [b_qh[cc][qb]], [qtb])
                    kt = it["kt"]; c0 = it["c0"]; n = it["n"]; idx0 = it["idx0"]; ph = it["ph"]
                    ktok = 2048 + (kt - 16) * 128 if kt >= 16 else kt * 128
                    qt, qtb = blkd["qz"][ph]
                    sp_, spb = psum()
                    S.op("pe", lambda e: e.matmul(sp_[:, 0:n], lhsT=KH[:, cc, ktok:ktok + 128], rhs=qt[:, c0:c0 + n],
                                                  start=True, stop=True), [b_kh[cc], qtb], [spb])
                    et, ebt = ES[ecnt[0] % len(ES)]
                    ecnt[0] += 1
                    it["et"] = (et, ebt)
                    S.op("act", lambda e: e.activation(out=et[:, 0:n], in_=sp_[:, 0:n], func=AF.Exp), [], [spb, ebt])
                    if idx0 is not None:
                        bt, bbt = BT[ph]
                        meng = "dve"
                        scnt[0] += 1
                        S.op(meng, lambda e: e.tensor_tensor(out=et[:, 0:n], in0=et[:, 0:n], in1=bt[:, idx0 * 64:idx0 * 64 + n],
                                                             op=ALU.mult), [bbt], [ebt])

                def stage_b(it):
                    blkd = it["blk"]; cc = blkd["cc"]; qb = blkd["qb"]; q0 = blkd["q0"]; N = blkd["N"]
                    kt = it["kt"]; c0 = it["c0"]; n = it["n"]; ph = it["ph"]
                    et, ebt = it["et"]
                    f = it["start"]
                    pp = po[ph]; pq = pd[ph]
                    S.op("pe", lambda e: e.matmul(pp[:, c0:c0 + n], lhsT=VHf[:, kt, cc * 128:(cc + 1) * 128], rhs=et[:, 0:n],
                                                  start=f, stop=False, skip_group_check=True), [ebt, b_vh[kt]], [pob[ph]])
                    S.op("pe", lambda e: e.matmul(pq[:, c0:c0 + n], lhsT=onesb[:, :], rhs=et[:, 0:n],
                                                  start=f, stop=False, skip_group_check=True), [ebt, B_const], [pdb[ph]])
                    if it["lastb"]:
                        for ph2 in range(2):
                            pl = slice(ph2 * 64, (ph2 + 1) * 64)
                            pq2 = pd[ph2]
                            S.op("act", lambda e: e.activation(out=RD[pl, 0:N], in_=pq2[pl, 0:N], func=AF.Ln), [], [pdb[ph2], b_rd])
                        S.op("act", lambda e: e.activation(out=RD[:, 0:N], in_=RD[:, 0:N], func=AF.Exp, scale=-1.0), [], [b_rd])
                        for ph2 in range(2):
                            pl = slice(ph2 * 64, (ph2 + 1) * 64)
                            pp2 = po[ph2]
                            S.op("dve", lambda e: e.tensor_tensor(out=QH[pl, cc, q0:q0 + N], in0=pp2[pl, 0:N], in1=RD[pl, 0:N],
                                                                  op=ALU.mult), [b_rd], [pob[ph2], b_qh[cc][qb]])

                LOOK = 6
                for i in range(len(items) + LOOK):
                    if i < len(items):
                        stage_a(items[i])
                    if i >= LOOK:
                        stage_b(items[i - LOOK])
                ps_pool[0] = list(range(8))
                WO = WS.rearrange("p k n -> p (k n)").rearrange("p (g k n) -> p g k n", g=2, k=4)
                S.dma("sp", lambda e, hh=hh: e.dma_start(out=WO, in_=scr_wo[j][:, :, 4 * hh:4 * hh + 4, :]), [], [b_ws])
                for dc in range(DC):
                    for si in range(nsub):
                        t0, nt = SUBS[si]
                        v = 2 if si == 4 else seq
                        pt, pb = psum()
                        for cc in range(4):
                            S.op("pe", lambda e, pt=pt, dc=dc, cc=cc, t0=t0, nt=nt: e.matmul(
                                pt[:, 0:nt], lhsT=WO[:, dc // 4, cc, (dc % 4) * 128:(dc % 4 + 1) * 128], rhs=QH[:, cc, t0:t0 + nt],
                                start=(cc == 0), stop=(cc == 3)), [b_ws, b_qh[cc][si]], [pb])
                        S.op("dve", lambda e, pt=pt, dc=dc, t0=t0, nt=nt, v=v: e.scalar_tensor_tensor(
                            out=XT[:, dc, t0:t0 + nt], in0=pt[:, 0:nt], scalar=modcol(l, 2, dc, v), in1=XT[:, dc, t0:t0 + nt],
                            op0=ALU.mult, op1=ALU.add), [B_mod], [pb, B_XT[dc][si]])

        MIX = {'pool': pool_layer, 'gmlp': gmlp_layer, 'na': na_layer}

        for seq in range(nseq):
            load_x(seq)
            for l in layers:
                kind = l % 3
                last = (l == DEPTH - 1)
                if kind == 0 and "na" in cfg["mixers"]:
                    MIX["na"](l, seq, last)
                elif kind == 1 and "gmlp" in cfg["mixers"]:
                    MIX["gmlp"](l, seq, last)
                elif kind == 2 and "pool" in cfg["mixers"]:
                    MIX["pool"](l, seq, last)
                if cfg["ffn"]:
                    ffn_layer(l, seq, last)
            store_x(seq)
        S.dma_fence("sp")
        stats = S.emit()
    return nc, stats


NB_TAB = 23
MIXER_BUILDERS = {}


def _prep_inputs(inputs, core):
    b0 = core * NSEQ
    m = {}
    m["x"] = np.ascontiguousarray(inputs["x"][b0:b0 + NSEQ])
    m["c"] = np.ascontiguousarray(inputs["c"][b0:b0 + NSEQ])
    m["ctx"] = np.ascontiguousarray(inputs["ctx"][b0:b0 + NSEQ])
    for k in ("c_ctx", "ada_w", "ada_b", "norm1_g", "norm2_g", "ffn_w_gate", "ffn_w_up", "ffn_w_down",
              "na_w_qkv", "na_w_o", "na_q_norm", "na_k_norm", "gm_w_in", "gm_b_in", "gm_ln_g", "gm_ln_b",
              "gm_w_s", "gm_b_s", "gm_w_out", "pool_w", "pool_scale"):
        m[k] = np.ascontiguousarray(inputs[k], dtype=np.float32)
    return m


def kernel(**inputs):
    inputs = {k: np.asarray(v) for k, v in inputs.items()}
    nc, _ = build_nc(CFG)
    extra = na_tables(inputs["na_rpb"])
    in_maps = []
    for core in range(8):
        m = _prep_inputs(inputs, core)
        m.update(extra)
        in_maps.append(m)
    res = run_bass_kernel_spmd(nc, in_maps, core_ids=list(range(8)))
    return np.concatenate([r["out"] for r in res.results], axis=0).astype(np.float32)


def na_tables(rpb):
    rpb = np.asarray(rpb, dtype=np.float32)
    tab = np.zeros((2, H, 128, NB_TAB, 64), np.float32)
    neg = np.full((128, NB_TAB, 64), -30000.0, np.float32)
    qcol = np.arange(64)
    cstart = np.clip(qcol - 8, 0, 48)
    kcol = np.arange(64)
    ok_col = (kcol[:, None] >= cstart[None, :]) & (kcol[:, None] < cstart[None, :] + 16)
    dcm = np.clip(kcol[:, None] - qcol[None, :] + 15, 0, 30)
    for idx in range(NB_TAB):
        if idx < 9:
            delta, lo, hi = 3 - idx, -4, 3
        else:
            delta, lo, hi = 6 - (idx - 9), -7, 7
        for krl in range(2):
            dr = delta + krl
            if not (lo <= dr <= hi):
                continue
            g = rpb[:, :, dr + 7, :][:, :, dcm]
            sel = np.broadcast_to(ok_col, g.shape)
            blk = tab[:, :, krl * 64:(krl + 1) * 64, idx, :]
            blk[sel] = g[sel]
            neg[krl * 64:(krl + 1) * 64, idx, :][ok_col] = 0.0
    return {"na_bias": np.ascontiguousarray(tab.reshape(2, H, 128, NB_TAB * 64)),
            "na_mask": np.ascontiguousarray(neg.reshape(128, NB_TAB * 64))}
```
